# Optimizing a Trainium2 kernel written in Bass

```python
import math
import jax, jax.numpy as jnp
from jax import lax
import numpy as np

D_MODEL = 1024
BATCH = 1
SEQ = 16384
DEPTH = 1

CHUNK = 64
Q_BLOCK = 128
HEAD_DIM = 64
N_HEADS_FOX = 8
WIDTH_FOX = N_HEADS_FOX * HEAD_DIM
N_HEADS_DIFF = 4
DIFF_V_DIM = 2 * HEAD_DIM
WIDTH_DIFF = N_HEADS_DIFF * DIFF_V_DIM
WIDTH_DIFF_QK = 2 * N_HEADS_DIFF * HEAD_DIM
MIX_WIDTH = WIDTH_FOX + WIDTH_DIFF
COL_SIZES = (WIDTH_FOX, WIDTH_FOX, WIDTH_FOX, WIDTH_FOX, N_HEADS_FOX,
             WIDTH_DIFF_QK, WIDTH_DIFF_QK, WIDTH_DIFF, WIDTH_DIFF)
IN_COLS = 4 * WIDTH_FOX + N_HEADS_FOX + 2 * WIDTH_DIFF_QK + 2 * WIDTH_DIFF
ROPE_THETA = 10000.0
EPS = 1e-6
FORGET_BIAS_OFFSET = 3.0

kernel_name = "hymba_fox_diffattn_streaming_block"


def rms_norm(x, g):
    xf = x.astype(jnp.float32)
    y = xf * lax.rsqrt(jnp.mean(xf * xf, axis=-1, keepdims=True) + EPS)
    return (y * g.astype(jnp.float32)).astype(x.dtype)


def split_columns(h):
    outs, start = [], 0
    for size in COL_SIZES:
        outs.append(h[..., start:start + size])
        start += size
    return outs


def to_heads(t, n_heads):
    b, s, w = t.shape
    return t.reshape(b, s, n_heads, w // n_heads).transpose(0, 2, 1, 3)


def from_heads(t):
    b, h, s, d = t.shape
    return t.transpose(0, 2, 1, 3).reshape(b, s, h * d)


def rope(x, pos):
    d = x.shape[-1]
    inv_freq = ROPE_THETA ** (-jnp.arange(0, d, 2, dtype=jnp.float32) / d)
    ang = pos.astype(jnp.float32)[:, None] * inv_freq[None, :]
    cos, sin = jnp.cos(ang), jnp.sin(ang)
    xf = x.astype(jnp.float32)
    x1, x2 = xf[..., : d // 2], xf[..., d // 2:]
    out = jnp.concatenate([x1 * cos - x2 * sin, x2 * cos + x1 * sin], axis=-1)
    return out.astype(x.dtype)


def blocks_to_seq(o):
    nb, b, h, qb, d = o.shape
    return o.transpose(1, 2, 0, 3, 4).reshape(b, h, nb * qb, d)


def forgetting_attention(q, k, v, log_f):
    seq = q.shape[2]
    scale = HEAD_DIM ** -0.5
    cum_f = jnp.cumsum(log_f, axis=-1)
    kpos = jnp.arange(seq)

    def one_block(i):
        start = i * Q_BLOCK
        qb = lax.dynamic_slice_in_dim(q, start, Q_BLOCK, axis=2)
        fq = lax.dynamic_slice_in_dim(cum_f, start, Q_BLOCK, axis=2)
        s = jnp.einsum('bhqd,bhkd->bhqk', qb, k, preferred_element_type=jnp.float32) * scale
        s = s + (fq[..., :, None] - cum_f[..., None, :])
        qpos = start + jnp.arange(Q_BLOCK)
        mask = kpos[None, :] <= qpos[:, None]
        s = jnp.where(mask, s, -jnp.inf)
        p = jax.nn.softmax(s, axis=-1)
        return jnp.einsum('bhqk,bhkd->bhqd', p.astype(v.dtype), v)

    out = lax.map(one_block, jnp.arange(seq // Q_BLOCK))
    return blocks_to_seq(out)


def differential_attention(q, k, v, lam, subln_g, lambda_init):
    b, h2, seq, _ = q.shape
    scale = HEAD_DIM ** -0.5
    kchunk = jnp.arange(seq) // CHUNK

    def one_block(i):
        start = i * Q_BLOCK
        qb = lax.dynamic_slice_in_dim(q, start, Q_BLOCK, axis=2)
        s = jnp.einsum('bhqd,bhkd->bhqk', qb, k, preferred_element_type=jnp.float32) * scale
        qchunk = (start + jnp.arange(Q_BLOCK)) // CHUNK
        mask = kchunk[None, :] <= qchunk[:, None]
        s = jnp.where(mask, s, -jnp.inf)
        p = jax.nn.softmax(s, axis=-1).reshape(b, h2 // 2, 2, Q_BLOCK, seq)
        a = p[:, :, 0] - lam * p[:, :, 1]
        return jnp.einsum('bhqk,bhkd->bhqd', a.astype(v.dtype), v)

    out = blocks_to_seq(lax.map(one_block, jnp.arange(seq // Q_BLOCK)))
    out = rms_norm(out, subln_g)
    return (out.astype(jnp.float32) * (1.0 - lambda_init)).astype(v.dtype)


def setup_inputs(seed: int = 0) -> dict:
    key = jax.random.key(seed)
    ks = jax.random.split(key, 12)
    x = jax.random.normal(ks[0], (BATCH, SEQ, D_MODEL), jnp.float32)
    norm_g = 1.0 + 0.02 * jax.random.normal(ks[1], (DEPTH, D_MODEL), jnp.float32)
    w_in = jax.random.normal(ks[2], (DEPTH, D_MODEL, IN_COLS), jnp.float32) * D_MODEL ** -0.5
    b_forget = FORGET_BIAS_OFFSET + 0.5 * jax.random.normal(ks[3], (DEPTH, N_HEADS_FOX), jnp.float32)
    lambda_q1 = 0.1 * jax.random.normal(ks[4], (DEPTH, HEAD_DIM), jnp.float32)
    lambda_k1 = 0.1 * jax.random.normal(ks[5], (DEPTH, HEAD_DIM), jnp.float32)
    lambda_q2 = 0.1 * jax.random.normal(ks[6], (DEPTH, HEAD_DIM), jnp.float32)
    lambda_k2 = 0.1 * jax.random.normal(ks[7], (DEPTH, HEAD_DIM), jnp.float32)
    subln_g = 1.0 + 0.02 * jax.random.normal(ks[8], (DEPTH, DIFF_V_DIM), jnp.float32)
    w_out = jax.random.normal(ks[9], (DEPTH, MIX_WIDTH, D_MODEL), jnp.float32) * MIX_WIDTH ** -0.5
    final_g = 1.0 + 0.02 * jax.random.normal(ks[10], (D_MODEL,), jnp.float32)
    return {"x": x, "norm_g": norm_g, "w_in": w_in, "b_forget": b_forget,
            "lambda_q1": lambda_q1, "lambda_k1": lambda_k1,
            "lambda_q2": lambda_q2, "lambda_k2": lambda_k2,
            "subln_g": subln_g, "w_out": w_out, "final_g": final_g}


def reference(x, norm_g, w_in, b_forget, lambda_q1, lambda_k1, lambda_q2, lambda_k2,
              subln_g, w_out, final_g):
    seq = x.shape[1]
    pos = jnp.arange(seq, dtype=jnp.int32)
    h = x
    for layer in range(DEPTH):
        lambda_init = 0.8 - 0.6 * math.exp(-0.3 * layer)
        u = rms_norm(h, norm_g[layer])
        proj = jnp.einsum('bsd,dc->bsc', u, w_in[layer])
        (fq, fk, fv, fg, fz, dq, dk, dv, dg) = split_columns(proj)

        z = fz.astype(jnp.float32) + b_forget[layer].astype(jnp.float32)
        log_f = jax.nn.log_sigmoid(z).transpose(0, 2, 1)
        y_fox = forgetting_attention(to_heads(fq, N_HEADS_FOX), to_heads(fk, N_HEADS_FOX),
                                     to_heads(fv, N_HEADS_FOX), log_f)
        y_fox = from_heads(y_fox) * jax.nn.silu(fg)

        lam = (jnp.exp(jnp.sum(lambda_q1[layer].astype(jnp.float32) * lambda_k1[layer].astype(jnp.float32)))
               - jnp.exp(jnp.sum(lambda_q2[layer].astype(jnp.float32) * lambda_k2[layer].astype(jnp.float32)))
               + lambda_init)
        qd = rope(to_heads(dq, 2 * N_HEADS_DIFF), pos)
        kd = rope(to_heads(dk, 2 * N_HEADS_DIFF), pos)
        vd = to_heads(dv, N_HEADS_DIFF)
        y_diff = differential_attention(qd, kd, vd, lam, subln_g[layer], lambda_init)
        y_diff = from_heads(y_diff) * jax.nn.silu(dg)

        mixed = jnp.concatenate([y_fox, y_diff], axis=-1)
        h = h + jnp.einsum('bsc,cd->bsd', mixed, w_out[layer])
    return rms_norm(h, final_g)
```

```python
import numpy as np
from contextlib import ExitStack
import concourse.bass as bass
import concourse.mybir as mybir
from concourse.bass_utils import run_bass_kernel_spmd

F32 = mybir.dt.float32
BF16 = mybir.dt.bfloat16
AF = mybir.ActivationFunctionType
ALU = mybir.AluOpType
AX = mybir.AxisListType

S = 16384
D = 1024
NCORES = 8
NQ = 2048
EPS = 1e-6
NEG = -30000.0
LAMBDA_INIT = 0.2
C_FQ, C_FK, C_FV, C_FG, C_FZ, C_DQ, C_DK, C_DV, C_DG = 0, 512, 1024, 1536, 2048, 2056, 2568, 3080, 3592


class Eng:
    def __init__(self, e, sem):
        self.e = e
        self.sem = sem
        self.n = 0
        self.seen = {}

    def mark(self, ins):
        ins.then_inc(self.sem, 1)
        self.n += 1
        return (self, self.n)

    def wait(self, tok):
        if tok is None:
            return
        prod, val = tok
        if val <= self.seen.get(id(prod), 0):
            return
        self.e.wait_ge(prod.sem, val)
        self.seen[id(prod)] = val


class DmaSem:
    def __init__(self, sem):
        self.sem = sem
        self.n = 0

    def add(self, ins):
        ins.then_inc(self.sem, 16)
        self.n += 16
        return (self, self.n)


def build(debug=False):
    nc = bass.Bass("TRN2", target_bir_lowering=False)
    scratch_kind = "ExternalOutput"

    def din(name, shape, dt=F32):
        return nc.dram_tensor(name, list(shape), dt, kind="ExternalInput").ap()

    xT = din("xT", [D, S])
    xoT = din("xoT", [D, NQ])
    xo = din("xo", [NQ, D])
    w_in = din("w_in", [D, 4104])
    w_sw = din("w_sw", [D, 1024])
    w_out = din("w_out", [D, D])
    norm_g = din("norm_g", [D])
    final_g = din("final_g", [D])
    b_forget = din("b_forget", [8])
    lam_in = din("lam_in", [4, 64])
    subln_g = din("subln_g", [128])
    cosT = din("cosT", [128, S])
    sinT = din("sinT", [128, S])
    cosq = din("cosq", [128, NQ])
    sinq = din("sinq", [128, NQ])
    consts = din("consts", [128, 4, 128])
    sel_in = din("sel", [128, 16])
    mfox_in = din("mfox", [128, 8, 128])
    mdiff_in = din("mdiff", [128, 8, 128])
    out = nc.dram_tensor("out", [NQ, D], F32, kind="ExternalOutput").ap()

    uT_scr = nc.dram_tensor("uT_scr", [D, S], BF16, kind=scratch_kind).ap()
    Qf_scr = nc.dram_tensor("Qf_scr", [8, 65, NQ], BF16, kind=scratch_kind).ap()
    Qd_scr = nc.dram_tensor("Qd_scr", [4, 128, NQ], BF16, kind=scratch_kind).ap()
    gate_scr = nc.dram_tensor("gate_scr", [NQ, D], BF16, kind=scratch_kind).ap()
    mixed_scr = nc.dram_tensor("mixed_scr", [NQ, D], BF16, kind=scratch_kind).ap()
    G_dbg = nc.dram_tensor("G_dbg", [128, 8, 128], F32, kind="ExternalOutput").ap() if debug else None
    K_dbg = nc.dram_tensor("K_dbg", [2, 128, 4096], BF16, kind="ExternalOutput").ap() if debug else None
    V_dbg = nc.dram_tensor("V_dbg", [128, 32, 2, 129], BF16, kind="ExternalOutput").ap() if debug else None

    uT_v = uT_scr.rearrange("(c p) s -> p c s", p=128)
    xT_v = xT.rearrange("(c p) s -> p c s", p=128)
    xoT_v = xoT.rearrange("(c p) s -> p c s", p=128)
    w_in_v = w_in.rearrange("(c p) n -> p c n", p=128)
    w_sw_v = w_sw.rearrange("(c p) n -> p c n", p=128)
    w_out_v = w_out.rearrange("(c p) n -> p c n", p=128)

    es = ExitStack()
    with es:
        def sb(name, shape, dt):
            return es.enter_context(nc.sbuf_tensor("sb_" + name, list(shape), dt))

        def newsem(name):
            return es.enter_context(nc.semaphore(name))

        PE = Eng(nc.tensor, newsem("s_pe"))
        ACT = Eng(nc.scalar, newsem("s_act"))
        DVE = Eng(nc.vector, newsem("s_dve"))
        POOL = Eng(nc.gpsimd, newsem("s_pool"))
        SP = Eng(nc.sync, newsem("s_sp"))

        ps_all = es.enter_context(nc.psum_tensor("ps_all", [128, 8, 512], F32))

        def full_barrier():
            toks = [(e_, e_.n) for e_ in (PE, ACT, DVE, POOL) if e_.n > 0]
            for e_ in (PE, ACT, DVE, POOL, SP):
                for tk_ in toks:
                    e_.wait(tk_)

        cst = sb("cst", [128, 4, 128], F32)
        ident32 = cst[:, 0, :]
        ones32 = cst[:, 1, :]
        tri32 = cst[:, 2, :]
        identb = sb("identb", [128, 128], BF16)
        permb = sb("permb", [128, 128], BF16)
        sel = sb("sel", [128, 16], F32)
        wz = sb("wz", [128, 8, 8], BF16)
        utb = [sb(f"utb{i}", [128, 8, 512], BF16) for i in range(3)]
        Pb = [sb(f"Pb{i}", [128, 2, 512], BF16) for i in range(3)]
        gsb = sb("gsb", [128, 16, 256], BF16)
        masks = sb("masks", [128, 2, 8, 128], BF16)
        gcol = sb("gcol", [128, 8], F32)
        negb = sb("negb", [128, 8], F32)
        lamt = sb("lamt", [128, 4, 64], F32)
        lamw = sb("lamw", [128, 8], F32)
        neglam = sb("neglam", [128, 1], F32)
        gsub = sb("gsub", [128, 128], F32)
        Gsb = sb("Gsb", [128, 8, 128], F32)
        ones_bf = sb("ones_bf", [128, 4], BF16)
        mst_scope = ExitStack()
        mstage = mst_scope.enter_context(nc.sbuf_tensor("sb_mstage", [128, 2, 8, 128], F32))

        ld_c = DmaSem(newsem("ld_c"))
        with nc.allow_non_contiguous_dma(reason="tiny one-time constant loads"):
            t = ld_c.add(nc.sync.dma_start(out=cst[:], in_=consts))
            t = ld_c.add(nc.sync.dma_start(out=sel[:], in_=sel_in))
            t = ld_c.add(nc.sync.dma_start(out=mstage[:, 0], in_=mfox_in))
            t = ld_c.add(nc.sync.dma_start(out=mstage[:, 1], in_=mdiff_in))
            t = ld_c.add(nc.sync.dma_start(out=gcol[:], in_=norm_g.rearrange("(c p) -> p c", p=128)))
            t = ld_c.add(nc.sync.dma_start(out=negb[:], in_=b_forget.partition_broadcast(128)))
            t = ld_c.add(nc.sync.dma_start(out=lamt[:], in_=lam_in.partition_broadcast(128)))
            t_const = ld_c.add(nc.sync.dma_start(out=gsub[:], in_=subln_g.partition_broadcast(128)))

        DVE.wait(t_const)
        nc.vector.tensor_copy(out=identb[:], in_=ident32)
        nc.vector.tensor_copy(out=permb[:], in_=cst[:, 3, :])
        nc.vector.tensor_copy(out=masks[:], in_=mstage[:])
        nc.vector.memset(ones_bf[:], 1.0)
        nc.vector.tensor_scalar(out=negb[:], in0=negb[:], scalar1=-1.0, scalar2=None, op0=ALU.mult)
        nc.vector.tensor_scalar(out=gsub[:], in0=gsub[:], scalar1=1.0 - LAMBDA_INIT, scalar2=None, op0=ALU.mult)
        nc.vector.tensor_tensor(out=lamt[:, 0, :], in0=lamt[:, 0, :], in1=lamt[:, 1, :], op=ALU.mult)
        t1 = DVE.mark(nc.vector.tensor_tensor(out=lamt[:, 2, :], in0=lamt[:, 2, :], in1=lamt[:, 3, :], op=ALU.mult))
        DVE.wait(t1)
        nc.vector.reduce_sum(out=lamw[:, 0:1], in_=lamt[:, 0, :], axis=AX.X)
        t1 = DVE.mark(nc.vector.reduce_sum(out=lamw[:, 1:2], in_=lamt[:, 2, :], axis=AX.X))
        ACT.wait(t1)
        t1 = ACT.mark(nc.scalar.activation(out=lamw[:, 2:4], in_=lamw[:, 0:2], func=AF.Exp))
        DVE.wait(t1)
        t1 = DVE.mark(nc.vector.tensor_tensor(out=lamw[:, 4:5], in0=lamw[:, 3:4], in1=lamw[:, 2:3], op=ALU.subtract))
        DVE.wait(t1)
        t_cdve = DVE.mark(nc.vector.tensor_scalar(out=neglam[:], in0=lamw[:, 4:5], scalar1=-LAMBDA_INIT, scalar2=None, op0=ALU.add))
        DVE.wait(t_cdve)
        for e_ in (SP, PE, ACT, POOL):
            e_.wait(t_cdve)
        mst_scope.close()

        ld_w = DmaSem(newsem("ld_w"))

        def load_w(dst, src_v, col0, n, scale=True):
            return ld_w.add(nc.gpsimd.dma_start(out=dst, in_=src_v[:, :, col0:col0 + n]))

        ncall = [0]

        def normalize(src_v, ntiles, dst_fn, after_fn, psbase, nxb=4):
            ncall[0] += 1
            nm = f"n{ncall[0]}"
            with ExitStack() as ns:
                NXB = nxb
                xt = [ns.enter_context(nc.sbuf_tensor(f"{nm}_xt{i}", [128, 8, 512], F32)) for i in range(NXB)]
                sq = [ns.enter_context(nc.sbuf_tensor(f"{nm}_sq{i}", [128, 8, 512], F32)) for i in range(2)]
                lnv = [ns.enter_context(nc.sbuf_tensor(f"{nm}_ln{i}", [128, 512], F32)) for i in range(2)]
                rb = [ns.enter_context(nc.sbuf_tensor(f"{nm}_rb{i}", [128, 512], F32)) for i in range(2)]
                ldx = [DmaSem(ns.enter_context(nc.semaphore(f"{nm}_ldx{i}"))) for i in range(NXB)]
                t_ld = [None] * ntiles
                t_sq = [None] * ntiles
                t_mm = [None] * ntiles
                t_ln = [None] * ntiles
                t_rb = [None] * ntiles
                t_nm = [None] * ntiles

                def issue_load(t):
                    xb_ = t % NXB
                    if t >= NXB:
                        SP.wait(t_nm[t - NXB])
                    t_ld[t] = ldx[xb_].add(nc.sync.dma_start(out=xt[xb_][:], in_=src_v[:, :, t * 512:(t + 1) * 512]))

                def do_sq(t):
                    b = t % 2
                    ACT.wait(t_ld[t])
                    if t >= 2:
                        ACT.wait(t_mm[t - 2])
                    t_sq[t] = ACT.mark(nc.scalar.activation(out=sq[b][:], in_=xt[t % NXB][:], func=AF.Square))
                    DVE.wait(t_sq[t])
                    tkd = DVE.mark(nc.vector.tensor_tensor(out=sq[b][:, 0:4, :], in0=sq[b][:, 0:4, :], in1=sq[b][:, 4:8, :], op=ALU.add))
                    PE.wait(tkd)
                    if t >= 2:
                        PE.wait(t_ln[t - 2])
                    for c in range(4):
                        ins = nc.tensor.matmul(ps_all[:, psbase + b, :], lhsT=ones32, rhs=sq[b][:, c, :],
                                               start=(c == 0), stop=(c == 3))
                    t_mm[t] = PE.mark(ins)

                def do_rest(t):
                    b = t % 2
                    ACT.wait(t_mm[t])
                    if t >= 2:
                        ACT.wait(t_nm[t - 2])
                    t_ln[t] = ACT.mark(nc.scalar.activation(out=lnv[b][:], in_=ps_all[:, psbase + b, :], func=AF.Ln,
                                                            bias=EPS, scale=1.0 / D))
                    ACT.wait(t_ln[t])
                    t_rb[t] = ACT.mark(nc.scalar.activation(out=rb[b][:], in_=lnv[b][:], func=AF.Exp, scale=-0.5))
                    DVE.wait(t_rb[t])
                    dst, pre = dst_fn(t)
                    for p in pre:
                        DVE.wait(p)
                    for c in range(8):
                        ins = nc.vector.scalar_tensor_tensor(out=dst[:, c, :], in0=xt[t % NXB][:, c, :], scalar=gcol[:, c:c + 1],
                                                             in1=rb[b][:, :], op0=ALU.mult, op1=ALU.mult)
                    t_nm[t] = DVE.mark(ins)
                    after_fn(t, t_nm[t])

                for t in range(min(NXB, ntiles)):
                    issue_load(t)
                do_sq(0)
                for t in range(ntiles):
                    if t + 1 < ntiles:
                        do_sq(t + 1)
                    do_rest(t)
                    if t + NXB < ntiles:
                        issue_load(t + NXB)
                if ncall[0] > 1:
                    full_barrier()
                return t_nm[ntiles - 1]

        t_wz = load_w(wz[:], w_in_v, C_FZ, 8)
        PE.wait(t_wz)
        pz_scope = ExitStack()
        zb = pz_scope.enter_context(nc.sbuf_tensor("sb_zb", [128, 128, 8], F32))
        st_u = [DmaSem(newsem(f"st_u{i}")) for i in range(2)]
        NT0 = S // 512
        with ExitStack() as p0:
            ut0 = [p0.enter_context(nc.sbuf_tensor(f"p0_ut{i}", [128, 8, 512], BF16)) for i in range(2)]
            t_st = [None] * NT0
            t_zmm = [None] * NT0
            t_zev = [None] * NT0

            def dst0(t):
                pre = []
                if t >= 2:
                    pre = [t_st[t - 2], t_zmm[t - 2]]
                return ut0[t % 2][:], pre

            def after0(t, tok):
                b = t % 2
                POOL.wait(tok)
                t_st[t] = st_u[b].add(nc.gpsimd.dma_start(out=uT_v[:, :, t * 512:(t + 1) * 512], in_=ut0[b][:]))
                PE.wait(tok)
                if t >= 1:
                    PE.wait(t_zev[t - 1])
                zps = ps_all[:, 2, 0:32].rearrange("p (s z) -> p s z", z=8)
                for s_ in range(4):
                    for c in range(8):
                        ins = nc.tensor.matmul(zps[:, s_, :], lhsT=ut0[b][:, c, s_ * 128:(s_ + 1) * 128], rhs=wz[:, c, :],
                                               start=(c == 0), stop=(c == 7))
                t_zmm[t] = PE.mark(ins)
                DVE.wait(t_zmm[t])
                t_zev[t] = DVE.mark(nc.vector.tensor_copy(out=zb[:, 4 * t:4 * t + 4, :], in_=zps))

            normalize(xT_v, NT0, dst0, after0, psbase=0)
            t_scr_done = [t_st[NT0 - 2], t_st[NT0 - 1]]
            t_z_done = t_zev[NT0 - 1]
            for tk_ in t_scr_done:
                POOL.wait(tk_)
            POOL.wait(t_z_done)
            t_p0_end = POOL.mark(nc.gpsimd.memset(ones_bf[:, 0:1], 1.0))
        for e_ in (DVE, PE, ACT, SP):
            e_.wait(t_p0_end)
        full_barrier()

        with ExitStack() as pg:
            et = pg.enter_context(nc.sbuf_tensor("g_e", [128, 8, 128], F32))
            spt = pg.enter_context(nc.sbuf_tensor("g_sp", [128, 8, 128], F32))
            sc = [pg.enter_context(nc.sbuf_tensor(f"g_sc{i}", [128, 8, 128], F32)) for i in range(2)]
            GT = pg.enter_context(nc.sbuf_tensor("g_GT", [128, 8, 128], F32))
            aq = pg.enter_context(nc.sbuf_tensor("g_aq", [16, 8, 128], BF16))
            ACT.wait(t_z_done)
            ACT.wait(t_cdve)
            for h in range(8):
                ins = nc.scalar.activation(out=et[:, h, :], in_=zb[:, :, h], func=AF.Exp, bias=negb[:, h:h + 1], scale=-1.0)
            tk = ACT.mark(ins)
            ACT.wait(tk)
            tk = ACT.mark(nc.scalar.activation(out=spt[:], in_=et[:], func=AF.Ln, bias=1.0, scale=1.0))
            PE.wait(tk)
            spf = spt[:].rearrange("p h k -> p (h k)")
            for hf in range(2):
                nc.tensor.matmul(ps_all[:, hf, :], lhsT=tri32, rhs=spf[:, hf * 512:(hf + 1) * 512], start=True, stop=True)
            for hf in range(2):
                ins = nc.tensor.matmul(ps_all[:, 2 + hf, :], lhsT=ones32, rhs=spf[:, hf * 512:(hf + 1) * 512], start=True, stop=True)
            tk = PE.mark(ins)
            DVE.wait(tk)
            tot_v = ps_all[:, 2:4, :].rearrange("p a (h k) -> p (a h) k", k=128)
            cs_v = ps_all[:, 0:2, :].rearrange("p a (h k) -> p (a h) k", k=128)
            tk = DVE.mark(nc.vector.tensor_copy(out=sc[0][:], in_=tot_v))
            cur = 0
            sh = 1
            while sh < 128:
                DVE.wait(tk)
                nc.vector.tensor_copy(out=sc[1 - cur][:, :, 0:sh], in_=sc[cur][:, :, 0:sh])
                tk = DVE.mark(nc.vector.tensor_tensor(out=sc[1 - cur][:, :, sh:128], in0=sc[cur][:, :, sh:128],
                                                      in1=sc[cur][:, :, 0:128 - sh], op=ALU.add))
                cur = 1 - cur
                sh *= 2
            DVE.wait(tk)
            tk = DVE.mark(nc.vector.tensor_tensor(out=sc[1 - cur][:], in0=sc[cur][:], in1=tot_v, op=ALU.subtract))
            DVE.wait(tk)
            t_G = DVE.mark(nc.vector.tensor_tensor(out=Gsb[:], in0=sc[1 - cur][:], in1=cs_v, op=ALU.add))
            PE.wait(t_G)
            for h in range(8):
                ins = nc.tensor.matmul(ps_all[:, 4 + h // 4, (h % 4) * 128:(h % 4 + 1) * 128], lhsT=Gsb[:, h, :], rhs=ident32,
                                       start=True, stop=True)
            tk = PE.mark(ins)
            DVE.wait(tk)
            tk = DVE.mark(nc.vector.tensor_copy(out=GT[:].rearrange("p h k -> p (h k)"),
                                                in_=ps_all[:, 4:6, :].rearrange("p a f -> p (a f)")))
            PE.wait(tk)
            GTf = GT[:].rearrange("p h k -> p (h k)")
            for hf in range(2):
                ins = nc.tensor.matmul(ps_all[0:16, 6 + hf, :], lhsT=sel[:, :], rhs=GTf[:, hf * 512:(hf + 1) * 512], start=True, stop=True)
            tk = PE.mark(ins)
            DVE.wait(tk)
            tk = DVE.mark(nc.vector.tensor_scalar(out=aq[:].rearrange("p h k -> p (h k)"),
                                                  in0=ps_all[0:16, 6:8, :].rearrange("p a f -> p (a f)"),
                                                  scalar1=-1.0, scalar2=None, op0=ALU.mult))
            POOL.wait(tk)
            st_aq = DmaSem(newsem("st_aq"))
            with nc.allow_non_contiguous_dma(reason="a_q rows 256B segments"):
                t_aq = st_aq.add(nc.gpsimd.dma_start(out=Qf_scr[:, 64, :].rearrange("h (j k) -> j h k", k=128), in_=aq[:]))
            if debug:
                t_gd = st_aq.add(nc.gpsimd.dma_start(out=G_dbg, in_=Gsb[:]))
                POOL.wait(t_gd)
            POOL.wait(t_aq)
            t_pg_end = POOL.mark(nc.gpsimd.memset(ones_bf[:, 0:1], 1.0))
        DVE.wait(t_pg_end)
        PE.wait(t_pg_end)
        ACT.wait(t_pg_end)
        SP.wait(t_pg_end)
        full_barrier()
        pz_scope.close()

        st_q = [DmaSem(newsem(f"st_q{i}")) for i in range(2)]
        st_gt = [DmaSem(newsem(f"st_gt{i}")) for i in range(2)]
        with ExitStack() as pa:
            uo = pa.enter_context(nc.sbuf_tensor("pa_uo", [128, 8, NQ], BF16))

            def dstA(t):
                return uo[:, :, t * 512:(t + 1) * 512], []

            wA = pa.enter_context(nc.sbuf_tensor("pa_w", [128, 8, 2560], BF16))
            load_w(wA[:, :, 0:512], w_in_v, C_FQ, 512)
            load_w(wA[:, :, 512:1024], w_in_v, C_DQ, 512)
            load_w(wA[:, :, 1024:1536], w_sw_v, 0, 512)
            load_w(wA[:, :, 1536:2048], w_in_v, C_FG, 512)
            t_wA = load_w(wA[:, :, 2048:2560], w_in_v, C_DG, 512)
            t_uo = normalize(xoT_v, NQ // 512, dstA, lambda t, tok: None, psbase=0, nxb=2)
            cq_sb = pa.enter_context(nc.sbuf_tensor("pa_cos", [128, NQ], F32))
            sq_sb = pa.enter_context(nc.sbuf_tensor("pa_sin", [128, NQ], F32))
            t_cs = ld_c.add(nc.sync.dma_start(out=cq_sb[:], in_=cosq))
            t_cs = ld_c.add(nc.sync.dma_start(out=sq_sb[:], in_=sinq))
            PE.wait(t_uo)
            PE.wait(t_wA)
            qst = [pa.enter_context(nc.sbuf_tensor(f"pa_qst{i}", [128, NQ], BF16)) for i in range(2)]
            tmp1 = pa.enter_context(nc.sbuf_tensor("pa_t1", [128, 512], F32))
            tmp2 = pa.enter_context(nc.sbuf_tensor("pa_t2", [128, 512], F32))
            t_qst_free = [None, None]
            t_ev = {}
            n_q = 0
            for h in range(8):
                b = n_q % 2
                n_q += 1
                DVE.wait(t_qst_free[b])
                for qt in range(4):
                    bank = qt
                    PE.wait(t_ev.get(bank))
                    for c in range(8):
                        ins = nc.tensor.matmul(ps_all[0:64, bank, :], lhsT=wA[:, c, 64 * h:64 * h + 64],
                                               rhs=uo[:, c, qt * 512:(qt + 1) * 512], start=(c == 0), stop=(c == 7))
                    tk = PE.mark(ins)
                    DVE.wait(tk)
                    t_ev[bank] = DVE.mark(nc.vector.tensor_scalar(out=qst[b][0:64, qt * 512:(qt + 1) * 512], in0=ps_all[0:64, bank, :],
                                                                  scalar1=0.125, scalar2=None, op0=ALU.mult))
                POOL.wait(t_ev[3])
                t_qst_free[b] = st_q[b].add(nc.gpsimd.dma_start(out=Qf_scr[h, 0:64, :], in_=qst[b][0:64, :]))
            DVE.wait(t_cs)
            for h in range(4):
                b = n_q % 2
                n_q += 1
                DVE.wait(t_qst_free[b])
                for qt in range(4):
                    ba, bb = 4 + 2 * (qt % 2), 5 + 2 * (qt % 2)
                    PE.wait(t_ev.get(ba))
                    for c in range(8):
                        nc.tensor.matmul(ps_all[:, ba, :], lhsT=wA[:, c, 512 + 128 * h:512 + 128 * h + 128],
                                         rhs=uo[:, c, qt * 512:(qt + 1) * 512], start=(c == 0), stop=(c == 7))
                    for c in range(8):
                        ins = nc.tensor.matmul(ps_all[:, bb, :], lhsT=wA[:, c, 1024 + 128 * h:1024 + 128 * h + 128],
                                               rhs=uo[:, c, qt * 512:(qt + 1) * 512], start=(c == 0), stop=(c == 7))
                    tk = PE.mark(ins)
                    DVE.wait(tk)
                    DVE.wait(t_ev.get("tmp"))
                    nc.vector.tensor_tensor(out=tmp1[:], in0=ps_all[:, ba, :], in1=cq_sb[:, qt * 512:(qt + 1) * 512], op=ALU.mult)
                    tk = DVE.mark(nc.vector.tensor_tensor(out=tmp2[:], in0=ps_all[:, bb, :], in1=sq_sb[:, qt * 512:(qt + 1) * 512], op=ALU.mult))
                    t_ev[ba] = tk
                    DVE.wait(tk)
                    tk = DVE.mark(nc.vector.tensor_tensor(out=tmp1[:], in0=tmp1[:], in1=tmp2[:], op=ALU.add))
                    DVE.wait(tk)
                    tk = DVE.mark(nc.vector.tensor_scalar(out=qst[b][:, qt * 512:(qt + 1) * 512], in0=tmp1[:], scalar1=0.125,
                                                          scalar2=None, op0=ALU.mult))
                    t_ev["tmp"] = tk
                POOL.wait(tk)
                t_qst_free[b] = st_q[b].add(nc.gpsimd.dma_start(out=Qd_scr[h, :, :], in_=qst[b][:, :]))
            gst = [pa.enter_context(nc.sbuf_tensor(f"pa_gst{i}", [128, D], BF16)) for i in range(2)]
            t_gst_free = [None, None]
            t_gev = [None, None]
            PE.wait(t_ev[4])
            PE.wait(t_ev[6])
            PE.wait(t_ev["tmp"])
            for j in range(16):
                b = j % 2
                for half in range(2):
                    bank = half
                    PE.wait(t_gev[half])
                    for c in range(8):
                        ins = nc.tensor.matmul(ps_all[:, bank, :], lhsT=uo[:, c, j * 128:(j + 1) * 128],
                                               rhs=wA[:, c, 1536 + half * 512:1536 + (half + 1) * 512], start=(c == 0), stop=(c == 7))
                    tk = PE.mark(ins)
                    ACT.wait(tk)
                    ACT.wait(t_gst_free[b])
                    tk = ACT.mark(nc.scalar.activation(out=gst[b][:, half * 512:(half + 1) * 512], in_=ps_all[:, bank, :], func=AF.Silu))
                    t_gev[half] = tk
                POOL.wait(tk)
                t_gst_free[b] = st_gt[b].add(nc.gpsimd.dma_start(out=gate_scr[j * 128:(j + 1) * 128, :], in_=gst[b][:]))
            POOL.wait(t_gst_free[0])
            POOL.wait(t_gst_free[1])
            POOL.wait(t_qst_free[0])
            POOL.wait(t_qst_free[1])
            t_pa_end = POOL.mark(nc.gpsimd.memset(ones_bf[:, 0:1], 1.0))
        for e_ in (DVE, PE, ACT, SP):
            e_.wait(t_pa_end)
        full_barrier()
        for tk in t_scr_done:
            SP.wait(tk)

        ld_u = [DmaSem(newsem(f"ld_u{i}")) for i in range(3)]
        ld_cs = [DmaSem(newsem(f"ld_cs{i}")) for i in range(3)]
        ld_g = DmaSem(newsem("ld_g"))
        st_m = DmaSem(newsem("st_m"))
        gstate = {"step": 0, "block": 0, "ut_n": 0}
        t_exp = {}
        t_pv = {}
        t_epi = {}
        t_ut_free = [None, None, None]
        t_gsb_free = [None]

        def run_kind(kind):
            nu = 4 if kind == "fox" else 2
            NS = 3 if kind == "fox" else 2
            LA = NS - 1
            with ExitStack() as gs:
                def gsbuf(name, shape, dt):
                    return gs.enter_context(nc.sbuf_tensor(f"{kind}_{name}", list(shape), dt))
                if kind == "fox":
                    Kt = [[gsbuf(f"K{p}{i}", [65 if i % 2 == 0 else 128, 4096], BF16) for i in range(4)] for p in range(2)]
                    Vt = [gsbuf(f"V{p}", [128, 32, 4, 65], BF16) for p in range(2)]
                    Qt = [gsbuf(f"Q{i}", [65 if i % 2 == 0 else 128, NQ], BF16) for i in range(4)]
                    acc = gsbuf("acc", [128, 4, 4, 260], F32)
                    wk = [gsbuf(f"wk{g_}", [128, 8, 256], BF16) for g_ in range(2)]
                    wv = [gsbuf(f"wv{g_}", [128, 8, 256], BF16) for g_ in range(2)]
                    tot_sb = gsbuf("tot", [128, 4, 65], F32)
                    rden = gsbuf("rden", [128, 4], F32)
                else:
                    Kt = [[gsbuf(f"K{p}{i}", [128, 4096], BF16) for i in range(2)] for p in range(2)]
                    Vt = [gsbuf(f"V{p}", [128, 32, 2, 129], BF16) for p in range(2)]
                    Qt = [gsbuf(f"Q{i}", [128, NQ], BF16) for i in range(2)]
                    acc = gsbuf("acc", [128, 2, 4, 8 * 129], F32)
                    wk = [gsbuf(f"wk{g_}", [128, 8, 256], BF16) for g_ in range(2)]
                    wv = [gsbuf(f"wv{g_}", [128, 8, 256], BF16) for g_ in range(2)]
                    cst_ = [gsbuf(f"cos{i}", [128, 512], F32) for i in range(3)]
                    snt_ = [gsbuf(f"sin{i}", [128, 512], F32) for i in range(3)]
                    tmpa = gsbuf("tmpa", [128, 512], F32)
                    tmpb = gsbuf("tmpb", [128, 512], F32)
                    khi = gsbuf("khi", [128, 512], BF16)
                    klo = gsbuf("klo", [128, 512], BF16)
                    tot_sb = gsbuf("tot", [128, 8, 129], F32)
                    rden = gsbuf("rden", [128, 12], F32)
                    ya = gsbuf("ya", [128, 4, 128], F32)
                    ysq = gsbuf("ysq", [128, 128], F32)
                    ssq = gsbuf("ssq", [128, 8], F32)
                    rs = gsbuf("rs", [128, 8], F32)
                    yb = gsbuf("yb", [128, 128], F32)

                for G_ in range(2):
                    if kind == "fox":
                        load_w(wk[G_][:], w_in_v, C_FK + 256 * G_, 256)
                        t_w = load_w(wv[G_][:], w_in_v, C_FV + 256 * G_, 256)
                    else:
                        load_w(wk[G_][:], w_in_v, C_DK + 256 * G_, 256)
                        t_w = load_w(wv[G_][:], w_in_v, C_DV + 256 * G_, 256)
                if kind == "fox":
                    for i in (1, 3):
                        nc.vector.memset(Qt[i][0:64, :], 0.0)
                    for p in range(2):
                        for i in range(4):
                            if i % 2 == 0:
                                nc.vector.memset(Kt[p][i][64:65, :], 1.0)
                            else:
                                nc.vector.memset(Kt[p][i][0:64, :], 1.0)
                        tk = DVE.mark(nc.vector.memset(Vt[p][:, :, :, 64:65], 1.0))
                else:
                    for p in range(2):
                        tk = DVE.mark(nc.vector.memset(Vt[p][:, :, :, 128:129], 1.0))
                t_init = tk
                PE.wait(t_w)
                PE.wait(t_init)
                SP.wait(t_init)

                def load_group_inputs(G):
                    if kind == "fox":
                        for i in range(4):
                            if i % 2 == 0:
                                ld_g.add(nc.sync.dma_start(out=Qt[i][:, :], in_=Qf_scr[4 * G + i, :, :]))
                            else:
                                ld_g.add(nc.sync.dma_start(out=Qt[i][64:128, :], in_=Qf_scr[4 * G + i, 0:64, :]))
                                ld_g.add(nc.sync.dma_start(out=Qt[i][63:64, :], in_=Qf_scr[4 * G + i, 64:65, :]))
                        return ld_g.add(nc.sync.dma_start(out=gsb[:], in_=gate_scr[:, 256 * G:256 * G + 256].rearrange("(j q) c -> q j c", q=128)))
                    for i in range(2):
                        ld_g.add(nc.sync.dma_start(out=Qt[i][:, :], in_=Qd_scr[2 * G + i, :, :]))
                    return ld_g.add(nc.sync.dma_start(out=gsb[:], in_=gate_scr[:, 512 + 256 * G:512 + 256 * G + 256].rearrange("(j q) c -> q j c", q=128)))

                t_cs_free = [None, None, None]
                p1_done = {}
                att_last_pv = {}
                pst8 = {"kb": None, "vb": [None, None], "b7": None, "tmp": None}

                def p1_items(n):
                    G, cq = divmod(n, 4)
                    par = n % 2
                    loads = {}
                    if n >= 2:
                        DVE.wait(att_last_pv[n - 2])

                    def issue_ut(t):
                        n = gstate["ut_n"]
                        gstate["ut_n"] += 1
                        b = n % 3
                        SP.wait(t_ut_free[b])
                        T = 8 * cq + t
                        tk_ = ld_u[b].add(nc.sync.dma_start(out=utb[b][:], in_=uT_v[:, :, T * 512:(T + 1) * 512]))
                        tcs = None
                        if kind == "diff":
                            SP.wait(t_cs_free[b])
                            ld_cs[b].add(nc.sync.dma_start(out=cst_[b][:], in_=cosT[:, T * 512:(T + 1) * 512]))
                            tcs = ld_cs[b].add(nc.sync.dma_start(out=snt_[b][:], in_=sinT[:, T * 512:(T + 1) * 512]))
                        loads[t] = (b, tk_, tcs)

                    issue_ut(0)
                    issue_ut(1)
                    last_ev = None
                    for t in range(8):
                        if t + 2 < 8:
                            issue_ut(t + 2)
                        b, tk_, tcs = loads[t]
                        u = utb[b]
                        if kind == "fox":
                            for ip in range(2):
                                PE.wait(tk_)
                                PE.wait(pst8["kb"])
                                for c in range(8):
                                    ins = nc.tensor.matmul(ps_all[:, 3, :], lhsT=wk[G][:, c, 128 * ip:128 * ip + 128], rhs=u[:, c, :],
                                                           start=(c == 0), stop=(c == 7))
                                    if c == 3:
                                        yield
                                tk = PE.mark(ins)
                                DVE.wait(tk)
                                nc.vector.tensor_copy(out=Kt[par][2 * ip][0:64, t * 512:(t + 1) * 512], in_=ps_all[0:64, 3, :])
                                pst8["kb"] = DVE.mark(nc.vector.tensor_copy(out=Kt[par][2 * ip + 1][64:128, t * 512:(t + 1) * 512],
                                                                            in_=ps_all[64:128, 3, :]))
                                yield
                            for s_ in range(4):
                                PE.wait(pst8["vb"][s_ // 2])
                                ov = ps_all[:, 6 + s_ // 2, (s_ % 2) * 256:(s_ % 2) * 256 + 256]
                                for c in range(8):
                                    ins = nc.tensor.matmul(ov, lhsT=u[:, c, s_ * 128:(s_ + 1) * 128], rhs=wv[G][:, c, :],
                                                           start=(c == 0), stop=(c == 7))
                                    if c == 3:
                                        yield
                                tk = PE.mark(ins)
                                DVE.wait(tk)
                                ins = nc.vector.tensor_copy(out=Vt[par][:, 4 * t + s_, :, 0:64], in_=ov.rearrange("p (h d) -> p h d", d=64))
                                tkd = DVE.mark(ins)
                                pst8["vb"][s_ // 2] = tkd
                                last_ev = tkd
                                if s_ == 3:
                                    t_ut_free[b] = tk
                                yield
                        else:
                            for i in range(2):
                                PE.wait(tk_)
                                PE.wait(pst8["b7"])
                                for c in range(8):
                                    ins = nc.tensor.matmul(ps_all[:, 7, :], lhsT=wk[G][:, c, 128 * i:128 * i + 128], rhs=u[:, c, :],
                                                           start=(c == 0), stop=(c == 7))
                                    if c == 3:
                                        yield
                                tk = PE.mark(ins)
                                DVE.wait(tk)
                                DVE.wait(tcs)
                                DVE.wait(pst8["tmp"])
                                nc.vector.tensor_tensor(out=tmpa[:], in0=ps_all[:, 7, :], in1=cst_[b][:], op=ALU.mult)
                                tkh = DVE.mark(nc.vector.tensor_copy(out=khi[:], in_=ps_all[:, 7, :]))
                                DVE.wait(tkh)
                                tkl = DVE.mark(nc.vector.tensor_tensor(out=klo[:], in0=ps_all[:, 7, :], in1=khi[:], op=ALU.subtract))
                                yield
                                PE.wait(tkl)
                                nc.tensor.matmul(ps_all[:, 7, :], lhsT=permb[:, :], rhs=khi[:], start=True, stop=False)
                                tk = PE.mark(nc.tensor.matmul(ps_all[:, 7, :], lhsT=permb[:, :], rhs=klo[:], start=False, stop=True))
                                DVE.wait(tk)
                                tk2 = DVE.mark(nc.vector.tensor_tensor(out=tmpb[:], in0=ps_all[:, 7, :], in1=snt_[b][:], op=ALU.mult))
                                pst8["b7"] = tk2
                                DVE.wait(tk2)
                                pst8["tmp"] = DVE.mark(nc.vector.tensor_tensor(out=Kt[par][i][:, t * 512:(t + 1) * 512], in0=tmpa[:], in1=tmpb[:], op=ALU.add))
                                yield
                            t_cs_free[b] = pst8["tmp"]
                            for sp in range(2):
                                PE.wait(pst8["b7"])
                                for s_ in (2 * sp, 2 * sp + 1):
                                    ov = ps_all[:, 7, (s_ % 2) * 256:(s_ % 2) * 256 + 256]
                                    for c in range(8):
                                        ins = nc.tensor.matmul(ov, lhsT=u[:, c, s_ * 128:(s_ + 1) * 128], rhs=wv[G][:, c, :],
                                                               start=(c == 0), stop=(c == 7))
                                tk = PE.mark(ins)
                                DVE.wait(tk)
                                for s_ in (2 * sp, 2 * sp + 1):
                                    ov = ps_all[:, 7, (s_ % 2) * 256:(s_ % 2) * 256 + 256]
                                    ins = nc.vector.tensor_copy(out=Vt[par][:, 4 * t + s_, :, 0:128], in_=ov.rearrange("p (h d) -> p h d", d=128))
                                pst8["b7"] = DVE.mark(ins)
                                last_ev = pst8["b7"]
                                if sp == 1:
                                    t_ut_free[b] = tk
                                yield
                    toks = [last_ev, pst8["kb"], pst8["tmp"]]
                    p1_done[n] = [x for x in toks if x is not None]

                for _ in p1_items(0):
                    pass

                t_fin = [None]
                deferred = []
                t_store = [None]
                for n in range(8):
                    G, cq = divmod(n, 4)
                    par_kv = n % 2
                    if cq == 0:
                        if G == 1:
                            SP.wait(t_pv[gstate["step"] - 1])
                            SP.wait(t_store[0])
                            DVE.wait(t_fin[0])
                        tg = load_group_inputs(G)
                        PE.wait(tg)
                        DVE.wait(tg)
                    for v in p1_done[n]:
                        PE.wait(v)
                    Kc = Kt[par_kv]
                    Vc = Vt[par_kv]
                    units = [(g, hl) for g in range(cq, 4) for hl in range(nu)]
                    steps = []
                    for (g, hl) in units:
                        for r in range(32):
                            steps.append((g, hl, r))
                    nst = len(steps)
                    base_step = gstate["step"]
                    base_block = gstate["block"]

                    def geom(g, r):
                        diag = (g == cq)
                        m0 = (r // 8) if diag else 0
                        return diag, m0

                    def emit_qk(si):
                        g, hl, r = steps[si]
                        gi = base_step + si
                        diag, m0 = geom(g, r)
                        c0 = 128 * m0
                        slot = gi % NS
                        PE.wait(t_exp.get(gi - NS))
                        if kind == "fox":
                            bank = slot
                            p0_, p1_ = (0, 65) if hl % 2 == 0 else (0, 128)
                            ins = nc.tensor.matmul(ps_all[:, bank, c0:512], lhsT=Kc[hl][p0_:p1_, r * 128:(r + 1) * 128],
                                                   rhs=Qt[hl][p0_:p1_, g * 512 + c0:(g + 1) * 512], start=True, stop=not diag)
                            if diag:
                                ins = nc.tensor.matmul(ps_all[:, bank, c0:c0 + 128], lhsT=identb[:, :], rhs=masks[:, 0, r % 8, :],
                                                       start=False, stop=True)
                        else:
                            for mp in range(2):
                                bank = 2 * slot + mp
                                ins = nc.tensor.matmul(ps_all[:, bank, c0:512], lhsT=Kc[hl][64 * mp:64 * mp + 64, r * 128:(r + 1) * 128],
                                                       rhs=Qt[hl][64 * mp:64 * mp + 64, g * 512 + c0:(g + 1) * 512], start=True, stop=not diag)
                            if diag:
                                for mp in range(2):
                                    bank = 2 * slot + mp
                                    ins = nc.tensor.matmul(ps_all[:, bank, c0:c0 + 128], lhsT=identb[:, :], rhs=masks[:, 1, r % 8, :],
                                                           start=False, stop=True)
                        tqk = PE.mark(ins)
                        pb = gi % 3
                        ACT.wait(tqk)
                        ACT.wait(t_pv.get(gi - 3))
                        if kind == "fox":
                            kb = 32 * cq + r
                            ins = nc.scalar.activation(out=Pb[pb][:, 0, c0:512], in_=ps_all[:, slot, c0:512], func=AF.Exp,
                                                       bias=Gsb[:, 4 * G + hl, kb:kb + 1], scale=1.0)
                        else:
                            ins = nc.scalar.activation(out=Pb[pb][:, :, c0:512], in_=ps_all[:, 2 * slot:2 * slot + 2, c0:512], func=AF.Exp)
                        t_exp[gi] = ACT.mark(ins)

                    def acc_view(blk_par, mp, m):
                        if kind == "fox":
                            return ps_all[:, 4 + blk_par, m * 65:(m + 1) * 65]
                        a = mp * 4 + m
                        return ps_all[:, 4 + a // 3, (a % 3) * 129:(a % 3 + 1) * 129]

                    def emit_pv(si):
                        g, hl, r = steps[si]
                        gi = base_step + si
                        blk = base_block + si // 32
                        diag, m0 = geom(g, r)
                        pb = gi % 3
                        PE.wait(t_exp[gi])
                        if r == 0:
                            PE.wait(t_epi.get(blk - 2 if kind == "fox" else blk - 1))
                        par = blk % 2
                        nmap = 1 if kind == "fox" else 2
                        for mp in range(nmap):
                            for m in range(m0, 4):
                                last_r = (8 * m + 7) if diag else 31
                                if kind == "fox":
                                    rhs = Vc[:, r, hl, 0:65]
                                else:
                                    rhs = Vc[:, r, hl, 0:129]
                                first_in_bank = (m == 0) if kind == "fox" else ((mp * 4 + m) % 3 == 0)
                                ins = nc.tensor.matmul(acc_view(par, mp, m), lhsT=Pb[pb][:, mp, 128 * m:128 * m + 128], rhs=rhs,
                                                       start=(r == 0 and first_in_bank), stop=(r == last_r),
                                                       skip_group_check=True)
                        t_pv[gi] = PE.mark(ins)
                        if r == 31:
                            emit_epilogue(g, hl, blk, gi)

                    def emit_epilogue(g, hl, blk, gi_last):
                        DVE.wait(t_pv[gi_last])
                        par = blk % 2
                        final = (g == cq)
                        if kind == "fox":
                            O = ps_all[:, 4 + par, 0:260]
                            A = acc[:, hl, g, :]
                            if not final:
                                if cq == 0:
                                    ins = nc.vector.tensor_copy(out=A, in_=O)
                                else:
                                    ins = nc.vector.tensor_tensor(out=A, in0=A, in1=O, op=ALU.add)
                                t_epi[blk] = DVE.mark(ins)
                                return
                            T3 = tot_sb[:].rearrange("p m d -> p (m d)")
                            DVE.wait(t_fin[0])
                            if g > 0:
                                tk = DVE.mark(nc.vector.tensor_tensor(out=T3, in0=A, in1=O, op=ALU.add))
                            else:
                                tk = DVE.mark(nc.vector.tensor_copy(out=T3, in_=O))
                            t_epi[blk] = tk
                            DVE.wait(tk)
                            tk = DVE.mark(nc.vector.reciprocal(out=rden[:, 0:4], in_=tot_sb[:, :, 64]))
                            DVE.wait(tk)
                            for m in range(4):
                                j = 4 * g + m
                                gv = gsb[:, j, 64 * hl:64 * hl + 64]
                                ins = nc.vector.scalar_tensor_tensor(out=gv, in0=tot_sb[:, m, 0:64], scalar=rden[:, m:m + 1], in1=gv,
                                                                     op0=ALU.mult, op1=ALU.mult)
                            t_fin[0] = DVE.mark(ins)
                            return
                        A = acc[:, hl, g, :]
                        if not final:
                            for bnk in range(3):
                                w_ = 387 if bnk < 2 else 258
                                Ab = A[:, bnk * 387:bnk * 387 + w_]
                                Ob = ps_all[:, 4 + bnk, 0:w_]
                                if cq == 0:
                                    ins = nc.vector.tensor_copy(out=Ab, in_=Ob)
                                else:
                                    ins = nc.vector.tensor_tensor(out=Ab, in0=Ab, in1=Ob, op=ALU.add)
                            t_epi[blk] = DVE.mark(ins)
                            return
                        Tf = tot_sb[:].rearrange("p a d -> p (a d)")
                        DVE.wait(t_fin[0])
                        for bnk in range(3):
                            w_ = 387 if bnk < 2 else 258
                            Tb = Tf[:, bnk * 387:bnk * 387 + w_]
                            Ob = ps_all[:, 4 + bnk, 0:w_]
                            if g > 0:
                                ins = nc.vector.tensor_tensor(out=Tb, in0=A[:, bnk * 387:bnk * 387 + w_], in1=Ob, op=ALU.add)
                            else:
                                ins = nc.vector.tensor_copy(out=Tb, in_=Ob)
                        tk = DVE.mark(ins)
                        t_epi[blk] = tk
                        DVE.wait(tk)
                        tk = DVE.mark(nc.vector.reciprocal(out=rden[:, 0:8], in_=tot_sb[:, :, 128]))
                        DVE.wait(tk)
                        tk = DVE.mark(nc.vector.tensor_scalar(out=rden[:, 8:12], in0=rden[:, 4:8], scalar1=neglam[:, 0:1], scalar2=None, op0=ALU.mult))
                        DVE.wait(tk)
                        for m in range(4):
                            tk = DVE.mark(nc.vector.tensor_scalar(out=yb[:], in0=tot_sb[:, 4 + m, 0:128], scalar1=rden[:, 8 + m:9 + m],
                                                                  scalar2=None, op0=ALU.mult))
                            DVE.wait(tk)
                            tk = DVE.mark(nc.vector.scalar_tensor_tensor(out=ya[:, m, :], in0=tot_sb[:, m, 0:128], scalar=rden[:, m:m + 1],
                                                                         in1=yb[:], op0=ALU.mult, op1=ALU.add))
                            DVE.wait(tk)
                            tk = DVE.mark(nc.vector.tensor_tensor(out=ysq[:], in0=ya[:, m, :], in1=ya[:, m, :], op=ALU.mult))
                            DVE.wait(tk)
                            tk = DVE.mark(nc.vector.reduce_sum(out=ssq[:, m:m + 1], in_=ysq[:], axis=AX.X))
                        tk_ss = tk

                        def part2(g=g, hl=hl, tk_ss=tk_ss):
                            epi_part2(g, hl, tk_ss)
                        deferred.append([4, part2])

                    def epi_part2(g, hl, tk):
                        ACT.wait(tk)
                        tk = ACT.mark(nc.scalar.activation(out=ssq[:, 4:8], in_=ssq[:, 0:4], func=AF.Ln, bias=EPS, scale=1.0 / 128))
                        ACT.wait(tk)
                        tk = ACT.mark(nc.scalar.activation(out=rs[:, 0:4], in_=ssq[:, 4:8], func=AF.Exp, scale=-0.5))
                        DVE.wait(tk)
                        for m in range(4):
                            j = 4 * g + m
                            gv = gsb[:, j, 128 * hl:128 * hl + 128]
                            tk = DVE.mark(nc.vector.scalar_tensor_tensor(out=yb[:], in0=ya[:, m, :], scalar=rs[:, m:m + 1], in1=gsub[:, :],
                                                                         op0=ALU.mult, op1=ALU.mult))
                            DVE.wait(tk)
                            tk = DVE.mark(nc.vector.tensor_tensor(out=gv, in0=yb[:], in1=gv, op=ALU.mult))
                            DVE.wait(tk)
                        t_fin[0] = tk

                    gen = p1_items(n + 1) if n < 7 else None
                    n_items = 96 if kind == "fox" else 64
                    kint = max(1, nst // n_items)
                    for si in range(min(LA, nst)):
                        emit_qk(si)
                    for si in range(nst):
                        if si + LA < nst:
                            emit_qk(si + LA)
                        emit_pv(si)
                        for d_ in list(deferred):
                            d_[0] -= 1
                            if d_[0] <= 0:
                                deferred.remove(d_)
                                d_[1]()
                        if gen is not None and si % kint == kint - 1:
                            next(gen, None)
                    if gen is not None:
                        for _ in gen:
                            pass
                    gstate["step"] += nst
                    gstate["block"] += len(units)
                    att_last_pv[n] = t_pv[gstate["step"] - 1]
                    while deferred:
                        deferred.pop(0)[1]()
                    if cq == 3:
                        POOL.wait(t_fin[0])
                        POOL.wait(t_epi[gstate["block"] - 1])
                        col0 = 256 * G if kind == "fox" else 512 + 256 * G
                        t_store[0] = st_m.add(nc.gpsimd.dma_start(out=mixed_scr[:, col0:col0 + 256].rearrange("(j q) c -> q j c", q=128), in_=gsb[:]))


                POOL.wait(t_store[0])
                POOL.wait(t_pv[gstate["step"] - 1])
                POOL.wait(t_exp[gstate["step"] - 1])
                t_end = POOL.mark(nc.gpsimd.memset(ones_bf[:, 0:1], 1.0))
            for e_ in (DVE, PE, ACT, SP):
                e_.wait(t_end)
            full_barrier()

        with nc.allow_non_contiguous_dma(reason="512B row segments of gate/mixed scratch"):
            run_kind("fox")
            run_kind("diff")

        with ExitStack() as pc:
            wo = pc.enter_context(nc.sbuf_tensor("pc_wo", [128, 8, D], BF16))
            fgbc = pc.enter_context(nc.sbuf_tensor("pc_fgbc", [128, D], F32))
            ld_f = DmaSem(pc.enter_context(nc.semaphore("pc_ldf")))
            with nc.allow_non_contiguous_dma(reason="broadcast load"):
                t_fg = ld_f.add(nc.sync.dma_start(out=fgbc[:], in_=final_g.partition_broadcast(128)))
            DVE.wait(t_fg)
            load_w(wo[:, :, 0:512], w_out_v, 0, 512, scale=False)
            t_wo = load_w(wo[:, :, 512:1024], w_out_v, 512, 512, scale=False)
            PE.wait(t_wo)
            msb = [pc.enter_context(nc.sbuf_tensor(f"pc_m{i}", [128, D], BF16)) for i in range(2)]
            xsb = [pc.enter_context(nc.sbuf_tensor(f"pc_x{i}", [128, D], F32)) for i in range(2)]
            mT = [pc.enter_context(nc.sbuf_tensor(f"pc_mT{i}", [128, 8, 128], BF16)) for i in range(2)]
            hsb = [pc.enter_context(nc.sbuf_tensor(f"pc_h{i}", [128, D], F32)) for i in range(2)]
            hsq = pc.enter_context(nc.sbuf_tensor("pc_hsq", [128, D], F32))
            osb = [pc.enter_context(nc.sbuf_tensor(f"pc_o{i}", [128, D], F32)) for i in range(2)]
            stat = pc.enter_context(nc.sbuf_tensor("pc_stat", [128, 16, 4], F32))
            ld_m = [DmaSem(pc.enter_context(nc.semaphore(f"pc_ldm{i}"))) for i in range(2)]
            st_o = [DmaSem(pc.enter_context(nc.semaphore(f"pc_sto{i}"))) for i in range(2)]
            pst = ps_all[:, 7, :].bitcast(BF16)
            t_tr_ev = [None] * 16
            t_h = [None] * 16
            t_o = [None] * 16
            t_st = [None] * 16
            t_mm = [None] * 16
            t_ldm = [None] * 16

            def issue_c(j):
                b = j % 2
                if j >= 2:
                    SP.wait(t_h[j - 2])
                    SP.wait(t_tr_ev[j - 2])
                ld_m[b].add(nc.sync.dma_start(out=msb[b][:], in_=mixed_scr[j * 128:(j + 1) * 128, :]))
                t_ldm[j] = ld_m[b].add(nc.sync.dma_start(out=xsb[b][:], in_=xo[j * 128:(j + 1) * 128, :]))

            t_red = [None] * 16

            def stage_a(j):
                b = j % 2
                PE.wait(t_ldm[j])
                if j >= 1:
                    PE.wait(t_tr_ev[j - 1])
                for c in range(8):
                    ins = nc.tensor.transpose(pst[:, c * 128:(c + 1) * 128], msb[b][:, c * 128:(c + 1) * 128], identb[:, :])
                tk = PE.mark(ins)
                DVE.wait(tk)
                if j >= 2:
                    DVE.wait(t_mm[j - 2])
                t_tr_ev[j] = DVE.mark(nc.vector.tensor_copy(out=mT[b][:].rearrange("p c q -> p (c q)"), in_=pst))
                PE.wait(t_tr_ev[j])
                if j >= 2:
                    PE.wait(t_h[j - 2])
                for half in range(2):
                    for c in range(8):
                        ins = nc.tensor.matmul(ps_all[:, 2 * b + half, :], lhsT=mT[b][:, c, :], rhs=wo[:, c, half * 512:(half + 1) * 512],
                                               start=(c == 0), stop=(c == 7))
                t_mm[j] = PE.mark(ins)

            def stage_b(j):
                b = j % 2
                DVE.wait(t_mm[j])
                if j >= 2:
                    DVE.wait(t_o[j - 2])
                t_h[j] = DVE.mark(nc.vector.tensor_tensor(out=hsb[b][:], in0=ps_all[:, 2 * b:2 * b + 2, :].rearrange("p a f -> p (a f)"),
                                                          in1=xsb[b][:], op=ALU.add))
                POOL.wait(t_h[j])
                if j >= 1:
                    POOL.wait(t_red[j - 1])
                tk = POOL.mark(nc.gpsimd.tensor_tensor(out=hsq[:], in0=hsb[b][:], in1=hsb[b][:], op=ALU.mult))
                DVE.wait(tk)
                t_red[j] = DVE.mark(nc.vector.reduce_sum(out=stat[:, j, 0:1], in_=hsq[:], axis=AX.X))
                ACT.wait(t_red[j])
                tk = ACT.mark(nc.scalar.activation(out=stat[:, j, 1:2], in_=stat[:, j, 0:1], func=AF.Ln, bias=EPS, scale=1.0 / D))
                ACT.wait(tk)
                tk = ACT.mark(nc.scalar.activation(out=stat[:, j, 2:3], in_=stat[:, j, 1:2], func=AF.Exp, scale=-0.5))
                DVE.wait(tk)
                if j >= 2:
                    DVE.wait(t_st[j - 2])
                t_o[j] = DVE.mark(nc.vector.scalar_tensor_tensor(out=osb[b][:], in0=hsb[b][:], scalar=stat[:, j, 2:3], in1=fgbc[:, :],
                                                                 op0=ALU.mult, op1=ALU.mult))
                POOL.wait(t_o[j])
                t_st[j] = st_o[b].add(nc.gpsimd.dma_start(out=out[j * 128:(j + 1) * 128, :], in_=osb[b][:]))
                if j + 2 < 16:
                    issue_c(j + 2)

            issue_c(0)
            issue_c(1)
            stage_a(0)
            for j in range(16):
                if j + 1 < 16:
                    stage_a(j + 1)
                stage_b(j)
            POOL.wait(t_st[14])
            POOL.wait(t_st[15])
            t_fin_all = POOL.mark(nc.gpsimd.memset(ones_bf[:, 0:1], 1.0))
            for e_ in (DVE, PE, ACT, SP):
                e_.wait(t_fin_all)
    return nc


_CACHE = {}


def _host_consts():
    if "c" in _CACHE:
        return _CACHE["c"]
    inv_freq = (10000.0 ** (-np.arange(0, 64, 2, dtype=np.float32) / 64.0)).astype(np.float32)
    pos = np.arange(S, dtype=np.float32)
    ang = pos[None, :] * inv_freq[:, None]
    cos = np.cos(ang).astype(np.float32)
    sin = np.sin(ang).astype(np.float32)
    cos64 = np.concatenate([cos, cos], axis=0)
    sin64 = np.concatenate([-sin, sin], axis=0)
    cosT = np.ascontiguousarray(np.concatenate([cos64, cos64], axis=0))
    sinT = np.ascontiguousarray(np.concatenate([sin64, sin64], axis=0))
    consts = np.zeros((128, 4, 128), np.float32)
    consts[:, 0, :] = np.eye(128, dtype=np.float32)
    consts[:, 1, :] = 1.0
    consts[:, 2, :] = np.triu(np.ones((128, 128), np.float32))
    for k_ in range(128):
        sw = k_ + 32 if (k_ % 64) < 32 else k_ - 32
        consts[sw, 3, k_] = 1.0
    k = np.arange(128)[:, None]
    q = np.arange(128)[None, :]
    fox_tri = np.where(k <= q, 0.0, NEG).astype(np.float32)
    diff_tri = np.where((k // 64) <= (q // 64), 0.0, NEG).astype(np.float32)
    per_core = []
    for c in range(NCORES):
        sel = np.zeros((128, 16), np.float32)
        for j in range(16):
            sel[8 * j + c, j] = 1.0
        mf = np.zeros((128, 8, 128), np.float32)
        md = np.zeros((128, 8, 128), np.float32)
        for idx in range(8):
            if idx == c:
                mf[:, idx, :] = fox_tri
                md[:, idx, :] = diff_tri
            elif idx > c:
                mf[:, idx, :] = NEG
                md[:, idx, :] = NEG
        rows = np.concatenate([np.arange((8 * j + c) * 128, (8 * j + c + 1) * 128) for j in range(16)])
        per_core.append(dict(sel=sel, mfox=mf, mdiff=md, rows=rows,
                             cosq=np.ascontiguousarray(cosT[:, rows]), sinq=np.ascontiguousarray(sinT[:, rows])))
    _CACHE["c"] = (cosT, sinT, consts, per_core)
    return _CACHE["c"]


def _swap_halves(w):
    w4 = w.reshape(w.shape[0], 8, 2, 32)
    return np.ascontiguousarray(w4[:, :, ::-1, :].reshape(w.shape[0], 512))


def make_in_maps(x, norm_g, w_in, b_forget, lambda_q1, lambda_k1, lambda_q2, lambda_k2, subln_g, w_out, final_g):
    cosT, sinT, consts, per_core = _host_consts()
    x2 = np.asarray(x, np.float32).reshape(S, D)
    xT = np.ascontiguousarray(x2.T)
    w_in0 = np.ascontiguousarray(np.asarray(w_in, np.float32)[0])
    w_sw = np.ascontiguousarray(np.concatenate([_swap_halves(w_in0[:, C_DQ:C_DQ + 512]),
                                                _swap_halves(w_in0[:, C_DK:C_DK + 512])], axis=1))
    lam_in = np.ascontiguousarray(np.stack([np.asarray(lambda_q1, np.float32)[0], np.asarray(lambda_k1, np.float32)[0],
                                            np.asarray(lambda_q2, np.float32)[0], np.asarray(lambda_k2, np.float32)[0]]))
    common = dict(xT=xT, w_in=w_in0, w_sw=w_sw, w_out=np.ascontiguousarray(np.asarray(w_out, np.float32)[0]),
                  norm_g=np.ascontiguousarray(np.asarray(norm_g, np.float32)[0]),
                  final_g=np.ascontiguousarray(np.asarray(final_g, np.float32)),
                  b_forget=np.ascontiguousarray(np.asarray(b_forget, np.float32)[0]), lam_in=lam_in,
                  subln_g=np.ascontiguousarray(np.asarray(subln_g, np.float32)[0]),
                  cosT=cosT, sinT=sinT, consts=consts)
    in_maps = []
    for c in range(NCORES):
        pc = per_core[c]
        m = dict(common)
        m["xoT"] = np.ascontiguousarray(xT[:, pc["rows"]])
        m["xo"] = np.ascontiguousarray(x2[pc["rows"], :])
        m["cosq"] = pc["cosq"]
        m["sinq"] = pc["sinq"]
        m["sel"] = pc["sel"]
        m["mfox"] = pc["mfox"]
        m["mdiff"] = pc["mdiff"]
        in_maps.append(m)
    return in_maps


def kernel(x, norm_g, w_in, b_forget, lambda_q1, lambda_k1, lambda_q2, lambda_k2, subln_g, w_out, final_g):
    in_maps = make_in_maps(x, norm_g, w_in, b_forget, lambda_q1, lambda_k1, lambda_q2, lambda_k2, subln_g, w_out, final_g)
    nc = build()
    res = run_bass_kernel_spmd(nc, in_maps, core_ids=list(range(NCORES)))
    _, _, _, per_core = _host_consts()
    outp = np.empty((S, D), np.float32)
    for c in range(NCORES):
        outp[per_core[c]["rows"], :] = np.asarray(res.results[c]["out"], np.float32).reshape(NQ, D)
    return outp.reshape(1, S, D)
```

```python
import numpy as np
from contextlib import ExitStack
import concourse.bass as bass
import concourse.mybir as mybir
from concourse.bass_utils import run_bass_kernel_spmd

F32 = mybir.dt.float32
BF16 = mybir.dt.bfloat16
AF = mybir.ActivationFunctionType
ALU = mybir.AluOpType
AX = mybir.AxisListType

S = 16384
D = 1024
NCORES = 8
NQ = 2048
EPS = 1e-6
NEG = -30000.0
LAMBDA_INIT = 0.2
C_FQ, C_FK, C_FV, C_FG, C_FZ, C_DQ, C_DK, C_DV, C_DG = 0, 512, 1024, 1536, 2048, 2056, 2568, 3080, 3592


class Eng:
    def __init__(self, e, sem):
        self.e = e
        self.sem = sem
        self.n = 0
        self.seen = {}

    def mark(self, ins):
        ins.then_inc(self.sem, 1)
        self.n += 1
        return (self, self.n)

    def wait(self, tok):
        if tok is None:
            return
        prod, val = tok
        if val <= self.seen.get(id(prod), 0):
            return
        self.e.wait_ge(prod.sem, val)
        self.seen[id(prod)] = val


class DmaSem:
    def __init__(self, sem):
        self.sem = sem
        self.n = 0

    def add(self, ins):
        ins.then_inc(self.sem, 16)
        self.n += 16
        return (self, self.n)


def build(debug=False):
    nc = bass.Bass("TRN2", target_bir_lowering=False)
    scratch_kind = "ExternalOutput"

    def din(name, shape, dt=F32):
        return nc.dram_tensor(name, list(shape), dt, kind="ExternalInput").ap()

    xT = din("xT", [D, S])
    xoT = din("xoT", [D, NQ])
    xo = din("xo", [NQ, D])
    w_in = din("w_in", [D, 4104])
    w_sw = din("w_sw", [D, 1024])
    w_out = din("w_out", [D, D])
    norm_g = din("norm_g", [D])
    final_g = din("final_g", [D])
    b_forget = din("b_forget", [8])
    lam_in = din("lam_in", [4, 64])
    subln_g = din("subln_g", [128])
    cosT = din("cosT", [128, S])
    sinT = din("sinT", [128, S])
    cosq = din("cosq", [128, NQ])
    sinq = din("sinq", [128, NQ])
    consts = din("consts", [128, 4, 128])
    sel_in = din("sel", [128, 16])
    mfox_in = din("mfox", [128, 8, 128])
    mdiff_in = din("mdiff", [128, 8, 128])
    out = nc.dram_tensor("out", [NQ, D], F32, kind="ExternalOutput").ap()

    uT_scr = nc.dram_tensor("uT_scr", [D, S], BF16, kind=scratch_kind).ap()
    Qf_scr = nc.dram_tensor("Qf_scr", [8, 65, NQ], BF16, kind=scratch_kind).ap()
    Qd_scr = nc.dram_tensor("Qd_scr", [4, 128, NQ], BF16, kind=scratch_kind).ap()
    gate_scr = nc.dram_tensor("gate_scr", [NQ, D], BF16, kind=scratch_kind).ap()
    mixed_scr = nc.dram_tensor("mixed_scr", [NQ, D], BF16, kind=scratch_kind).ap()
    G_dbg = nc.dram_tensor("G_dbg", [128, 8, 128], F32, kind="ExternalOutput").ap() if debug else None
    K_dbg = nc.dram_tensor("K_dbg", [2, 128, 4096], BF16, kind="ExternalOutput").ap() if debug else None
    V_dbg = nc.dram_tensor("V_dbg", [128, 32, 2, 129], BF16, kind="ExternalOutput").ap() if debug else None

    uT_v = uT_scr.rearrange("(c p) s -> p c s", p=128)
    xT_v = xT.rearrange("(c p) s -> p c s", p=128)
    xoT_v = xoT.rearrange("(c p) s -> p c s", p=128)
    w_in_v = w_in.rearrange("(c p) n -> p c n", p=128)
    w_sw_v = w_sw.rearrange("(c p) n -> p c n", p=128)
    w_out_v = w_out.rearrange("(c p) n -> p c n", p=128)

    es = ExitStack()
    with es:
        def sb(name, shape, dt):
            return es.enter_context(nc.sbuf_tensor("sb_" + name, list(shape), dt))

        def newsem(name):
            return es.enter_context(nc.semaphore(name))

        PE = Eng(nc.tensor, newsem("s_pe"))
        ACT = Eng(nc.scalar, newsem("s_act"))
        DVE = Eng(nc.vector, newsem("s_dve"))
        POOL = Eng(nc.gpsimd, newsem("s_pool"))
        SP = Eng(nc.sync, newsem("s_sp"))

        ps_all = es.enter_context(nc.psum_tensor("ps_all", [128, 8, 512], F32))

        def full_barrier():
            toks = [(e_, e_.n) for e_ in (PE, ACT, DVE, POOL) if e_.n > 0]
            for e_ in (PE, ACT, DVE, POOL, SP):
                for tk_ in toks:
                    e_.wait(tk_)

        cst = sb("cst", [128, 4, 128], F32)
        ident32 = cst[:, 0, :]
        ones32 = cst[:, 1, :]
        tri32 = cst[:, 2, :]
        identb = sb("identb", [128, 128], BF16)
        permb = sb("permb", [128, 128], BF16)
        sel = sb("sel", [128, 16], F32)
        wz = sb("wz", [128, 8, 8], BF16)
        utb = [sb(f"utb{i}", [128, 8, 512], BF16) for i in range(3)]
        Pb = [sb(f"Pb{i}", [128, 2, 512], BF16) for i in range(3)]
        gsb = sb("gsb", [128, 16, 256], BF16)
        masks = sb("masks", [128, 2, 8, 128], BF16)
        gcol = sb("gcol", [128, 8], F32)
        negb = sb("negb", [128, 8], F32)
        lamt = sb("lamt", [128, 4, 64], F32)
        lamw = sb("lamw", [128, 8], F32)
        neglam = sb("neglam", [128, 1], F32)
        gsub = sb("gsub", [128, 128], F32)
        Gsb = sb("Gsb", [128, 8, 128], F32)
        ones_bf = sb("ones_bf", [128, 4], BF16)
        mst_scope = ExitStack()
        mstage = mst_scope.enter_context(nc.sbuf_tensor("sb_mstage", [128, 2, 8, 128], F32))

        ld_c = DmaSem(newsem("ld_c"))
        with nc.allow_non_contiguous_dma(reason="tiny one-time constant loads"):
            t = ld_c.add(nc.sync.dma_start(out=cst[:], in_=consts))
            t = ld_c.add(nc.sync.dma_start(out=sel[:], in_=sel_in))
            t = ld_c.add(nc.sync.dma_start(out=mstage[:, 0], in_=mfox_in))
            t = ld_c.add(nc.sync.dma_start(out=mstage[:, 1], in_=mdiff_in))
            t = ld_c.add(nc.sync.dma_start(out=gcol[:], in_=norm_g.rearrange("(c p) -> p c", p=128)))
            t = ld_c.add(nc.sync.dma_start(out=negb[:], in_=b_forget.partition_broadcast(128)))
            t = ld_c.add(nc.sync.dma_start(out=lamt[:], in_=lam_in.partition_broadcast(128)))
            t_const = ld_c.add(nc.sync.dma_start(out=gsub[:], in_=subln_g.partition_broadcast(128)))

        DVE.wait(t_const)
        nc.vector.tensor_copy(out=identb[:], in_=ident32)
        nc.vector.tensor_copy(out=permb[:], in_=cst[:, 3, :])
        nc.vector.tensor_copy(out=masks[:], in_=mstage[:])
        nc.vector.memset(ones_bf[:], 1.0)
        nc.vector.tensor_scalar(out=negb[:], in0=negb[:], scalar1=-1.0, scalar2=None, op0=ALU.mult)
        nc.vector.tensor_scalar(out=gsub[:], in0=gsub[:], scalar1=1.0 - LAMBDA_INIT, scalar2=None, op0=ALU.mult)
        nc.vector.tensor_tensor(out=lamt[:, 0, :], in0=lamt[:, 0, :], in1=lamt[:, 1, :], op=ALU.mult)
        t1 = DVE.mark(nc.vector.tensor_tensor(out=lamt[:, 2, :], in0=lamt[:, 2, :], in1=lamt[:, 3, :], op=ALU.mult))
        DVE.wait(t1)
        nc.vector.reduce_sum(out=lamw[:, 0:1], in_=lamt[:, 0, :], axis=AX.X)
        t1 = DVE.mark(nc.vector.reduce_sum(out=lamw[:, 1:2], in_=lamt[:, 2, :], axis=AX.X))
        ACT.wait(t1)
        t1 = ACT.mark(nc.scalar.activation(out=lamw[:, 2:4], in_=lamw[:, 0:2], func=AF.Exp))
        DVE.wait(t1)
        t1 = DVE.mark(nc.vector.tensor_tensor(out=lamw[:, 4:5], in0=lamw[:, 3:4], in1=lamw[:, 2:3], op=ALU.subtract))
        DVE.wait(t1)
        t_cdve = DVE.mark(nc.vector.tensor_scalar(out=neglam[:], in0=lamw[:, 4:5], scalar1=-LAMBDA_INIT, scalar2=None, op0=ALU.add))
        DVE.wait(t_cdve)
        for e_ in (SP, PE, ACT, POOL):
            e_.wait(t_cdve)
        mst_scope.close()

        ld_w = DmaSem(newsem("ld_w"))

        def load_w(dst, src_v, col0, n, scale=True):
            return ld_w.add(nc.gpsimd.dma_start(out=dst, in_=src_v[:, :, col0:col0 + n]))

        ncall = [0]

        def normalize(src_v, ntiles, dst_fn, after_fn, psbase, nxb=4):
            ncall[0] += 1
            nm = f"n{ncall[0]}"
            with ExitStack() as ns:
                NXB = nxb
                xt = [ns.enter_context(nc.sbuf_tensor(f"{nm}_xt{i}", [128, 8, 512], F32)) for i in range(NXB)]
                sq = [ns.enter_context(nc.sbuf_tensor(f"{nm}_sq{i}", [128, 8, 512], F32)) for i in range(2)]
                lnv = [ns.enter_context(nc.sbuf_tensor(f"{nm}_ln{i}", [128, 512], F32)) for i in range(2)]
                rb = [ns.enter_context(nc.sbuf_tensor(f"{nm}_rb{i}", [128, 512], F32)) for i in range(2)]
                ldx = [DmaSem(ns.enter_context(nc.semaphore(f"{nm}_ldx{i}"))) for i in range(NXB)]
                t_ld = [None] * ntiles
                t_sq = [None] * ntiles
                t_mm = [None] * ntiles
                t_ln = [None] * ntiles
                t_rb = [None] * ntiles
                t_nm = [None] * ntiles

                def issue_load(t):
                    xb_ = t % NXB
                    if t >= NXB:
                        SP.wait(t_nm[t - NXB])
                    t_ld[t] = ldx[xb_].add(nc.sync.dma_start(out=xt[xb_][:], in_=src_v[:, :, t * 512:(t + 1) * 512]))

                def do_sq(t):
                    b = t % 2
                    ACT.wait(t_ld[t])
                    if t >= 2:
                        ACT.wait(t_mm[t - 2])
                    t_sq[t] = ACT.mark(nc.scalar.activation(out=sq[b][:], in_=xt[t % NXB][:], func=AF.Square))
                    PE.wait(t_sq[t])
                    if t >= 2:
                        PE.wait(t_ln[t - 2])
                    for c in range(8):
                        ins = nc.tensor.matmul(ps_all[:, psbase + b, :], lhsT=ones32, rhs=sq[b][:, c, :],
                                               start=(c == 0), stop=(c == 7))
                    t_mm[t] = PE.mark(ins)

                def do_rest(t):
                    b = t % 2
                    ACT.wait(t_mm[t])
                    if t >= 2:
                        ACT.wait(t_nm[t - 2])
                    t_ln[t] = ACT.mark(nc.scalar.activation(out=lnv[b][:], in_=ps_all[:, psbase + b, :], func=AF.Ln,
                                                            bias=EPS, scale=1.0 / D))
                    ACT.wait(t_ln[t])
                    t_rb[t] = ACT.mark(nc.scalar.activation(out=rb[b][:], in_=lnv[b][:], func=AF.Exp, scale=-0.5))
                    DVE.wait(t_rb[t])
                    dst, pre = dst_fn(t)
                    for p in pre:
                        DVE.wait(p)
                    for c in range(8):
                        ins = nc.vector.scalar_tensor_tensor(out=dst[:, c, :], in0=xt[t % NXB][:, c, :], scalar=gcol[:, c:c + 1],
                                                             in1=rb[b][:, :], op0=ALU.mult, op1=ALU.mult)
                    t_nm[t] = DVE.mark(ins)
                    after_fn(t, t_nm[t])

                for t in range(min(NXB, ntiles)):
                    issue_load(t)
                do_sq(0)
                for t in range(ntiles):
                    if t + 1 < ntiles:
                        do_sq(t + 1)
                    do_rest(t)
                    if t + NXB < ntiles:
                        issue_load(t + NXB)
                if ncall[0] > 1:
                    full_barrier()
                return t_nm[ntiles - 1]

        t_wz = load_w(wz[:], w_in_v, C_FZ, 8)
        PE.wait(t_wz)
        pz_scope = ExitStack()
        zb = pz_scope.enter_context(nc.sbuf_tensor("sb_zb", [128, 128, 8], F32))
        st_u = [DmaSem(newsem(f"st_u{i}")) for i in range(2)]
        NT0 = S // 512
        with ExitStack() as p0:
            ut0 = [p0.enter_context(nc.sbuf_tensor(f"p0_ut{i}", [128, 8, 512], BF16)) for i in range(2)]
            t_st = [None] * NT0
            t_zmm = [None] * NT0
            t_zev = [None] * NT0

            def dst0(t):
                pre = []
                if t >= 2:
                    pre = [t_st[t - 2], t_zmm[t - 2]]
                return ut0[t % 2][:], pre

            def after0(t, tok):
                b = t % 2
                POOL.wait(tok)
                t_st[t] = st_u[b].add(nc.gpsimd.dma_start(out=uT_v[:, :, t * 512:(t + 1) * 512], in_=ut0[b][:]))
                PE.wait(tok)
                if t >= 1:
                    PE.wait(t_zev[t - 1])
                zps = ps_all[:, 2, 0:32].rearrange("p (s z) -> p s z", z=8)
                for s_ in range(4):
                    for c in range(8):
                        ins = nc.tensor.matmul(zps[:, s_, :], lhsT=ut0[b][:, c, s_ * 128:(s_ + 1) * 128], rhs=wz[:, c, :],
                                               start=(c == 0), stop=(c == 7))
                t_zmm[t] = PE.mark(ins)
                DVE.wait(t_zmm[t])
                t_zev[t] = DVE.mark(nc.vector.tensor_copy(out=zb[:, 4 * t:4 * t + 4, :], in_=zps))

            normalize(xT_v, NT0, dst0, after0, psbase=0, nxb=6)
            t_scr_done = [t_st[NT0 - 2], t_st[NT0 - 1]]
            t_z_done = t_zev[NT0 - 1]
            for tk_ in t_scr_done:
                POOL.wait(tk_)
            POOL.wait(t_z_done)
            t_p0_end = POOL.mark(nc.gpsimd.memset(ones_bf[:, 0:1], 1.0))
        for e_ in (DVE, PE, ACT, SP):
            e_.wait(t_p0_end)
        full_barrier()

        with ExitStack() as pg:
            et = pg.enter_context(nc.sbuf_tensor("g_e", [128, 8, 128], F32))
            spt = pg.enter_context(nc.sbuf_tensor("g_sp", [128, 8, 128], F32))
            sc = [pg.enter_context(nc.sbuf_tensor(f"g_sc{i}", [128, 8, 128], F32)) for i in range(2)]
            GT = pg.enter_context(nc.sbuf_tensor("g_GT", [128, 8, 128], F32))
            aq = pg.enter_context(nc.sbuf_tensor("g_aq", [16, 8, 128], BF16))
            ACT.wait(t_z_done)
            ACT.wait(t_cdve)
            for h in range(8):
                ins = nc.scalar.activation(out=et[:, h, :], in_=zb[:, :, h], func=AF.Exp, bias=negb[:, h:h + 1], scale=-1.0)
            tk = ACT.mark(ins)
            ACT.wait(tk)
            tk = ACT.mark(nc.scalar.activation(out=spt[:], in_=et[:], func=AF.Ln, bias=1.0, scale=1.0))
            PE.wait(tk)
            spf = spt[:].rearrange("p h k -> p (h k)")
            for hf in range(2):
                nc.tensor.matmul(ps_all[:, hf, :], lhsT=tri32, rhs=spf[:, hf * 512:(hf + 1) * 512], start=True, stop=True)
            for hf in range(2):
                ins = nc.tensor.matmul(ps_all[:, 2 + hf, :], lhsT=ones32, rhs=spf[:, hf * 512:(hf + 1) * 512], start=True, stop=True)
            tk = PE.mark(ins)
            DVE.wait(tk)
            tot_v = ps_all[:, 2:4, :].rearrange("p a (h k) -> p (a h) k", k=128)
            cs_v = ps_all[:, 0:2, :].rearrange("p a (h k) -> p (a h) k", k=128)
            tk = DVE.mark(nc.vector.tensor_copy(out=sc[0][:], in_=tot_v))
            cur = 0
            sh = 1
            while sh < 128:
                DVE.wait(tk)
                nc.vector.tensor_copy(out=sc[1 - cur][:, :, 0:sh], in_=sc[cur][:, :, 0:sh])
                tk = DVE.mark(nc.vector.tensor_tensor(out=sc[1 - cur][:, :, sh:128], in0=sc[cur][:, :, sh:128],
                                                      in1=sc[cur][:, :, 0:128 - sh], op=ALU.add))
                cur = 1 - cur
                sh *= 2
            DVE.wait(tk)
            tk = DVE.mark(nc.vector.tensor_tensor(out=sc[1 - cur][:], in0=sc[cur][:], in1=tot_v, op=ALU.subtract))
            DVE.wait(tk)
            t_G = DVE.mark(nc.vector.tensor_tensor(out=Gsb[:], in0=sc[1 - cur][:], in1=cs_v, op=ALU.add))
            PE.wait(t_G)
            for h in range(8):
                ins = nc.tensor.matmul(ps_all[:, 4 + h // 4, (h % 4) * 128:(h % 4 + 1) * 128], lhsT=Gsb[:, h, :], rhs=ident32,
                                       start=True, stop=True)
            tk = PE.mark(ins)
            DVE.wait(tk)
            tk = DVE.mark(nc.vector.tensor_copy(out=GT[:].rearrange("p h k -> p (h k)"),
                                                in_=ps_all[:, 4:6, :].rearrange("p a f -> p (a f)")))
            PE.wait(tk)
            GTf = GT[:].rearrange("p h k -> p (h k)")
            for hf in range(2):
                ins = nc.tensor.matmul(ps_all[0:16, 6 + hf, :], lhsT=sel[:, :], rhs=GTf[:, hf * 512:(hf + 1) * 512], start=True, stop=True)
            tk = PE.mark(ins)
            DVE.wait(tk)
            tk = DVE.mark(nc.vector.tensor_scalar(out=aq[:].rearrange("p h k -> p (h k)"),
                                                  in0=ps_all[0:16, 6:8, :].rearrange("p a f -> p (a f)"),
                                                  scalar1=-1.0, scalar2=None, op0=ALU.mult))
            POOL.wait(tk)
            st_aq = DmaSem(newsem("st_aq"))
            with nc.allow_non_contiguous_dma(reason="a_q rows 256B segments"):
                t_aq = st_aq.add(nc.gpsimd.dma_start(out=Qf_scr[:, 64, :].rearrange("h (j k) -> j h k", k=128), in_=aq[:]))
            if debug:
                t_gd = st_aq.add(nc.gpsimd.dma_start(out=G_dbg, in_=Gsb[:]))
                POOL.wait(t_gd)
            POOL.wait(t_aq)
            t_pg_end = POOL.mark(nc.gpsimd.memset(ones_bf[:, 0:1], 1.0))
        DVE.wait(t_pg_end)
        PE.wait(t_pg_end)
        ACT.wait(t_pg_end)
        SP.wait(t_pg_end)
        full_barrier()
        pz_scope.close()

        st_q = [DmaSem(newsem(f"st_q{i}")) for i in range(2)]
        st_gt = [DmaSem(newsem(f"st_gt{i}")) for i in range(2)]
        with ExitStack() as pa:
            uo = pa.enter_context(nc.sbuf_tensor("pa_uo", [128, 8, NQ], BF16))

            def dstA(t):
                return uo[:, :, t * 512:(t + 1) * 512], []

            wA = pa.enter_context(nc.sbuf_tensor("pa_w", [128, 8, 2560], BF16))
            load_w(wA[:, :, 0:512], w_in_v, C_FQ, 512)
            load_w(wA[:, :, 512:1024], w_in_v, C_DQ, 512)
            load_w(wA[:, :, 1024:1536], w_sw_v, 0, 512)
            load_w(wA[:, :, 1536:2048], w_in_v, C_FG, 512)
            t_wA = load_w(wA[:, :, 2048:2560], w_in_v, C_DG, 512)
            t_uo = normalize(xoT_v, NQ // 512, dstA, lambda t, tok: None, psbase=0, nxb=2)
            cq_sb = pa.enter_context(nc.sbuf_tensor("pa_cos", [128, NQ], F32))
            sq_sb = pa.enter_context(nc.sbuf_tensor("pa_sin", [128, NQ], F32))
            t_cs = ld_c.add(nc.sync.dma_start(out=cq_sb[:], in_=cosq))
            t_cs = ld_c.add(nc.sync.dma_start(out=sq_sb[:], in_=sinq))
            PE.wait(t_uo)
            PE.wait(t_wA)
            qst = [pa.enter_context(nc.sbuf_tensor(f"pa_qst{i}", [128, NQ], BF16)) for i in range(2)]
            tmp1 = pa.enter_context(nc.sbuf_tensor("pa_t1", [128, 512], F32))
            tmp2 = pa.enter_context(nc.sbuf_tensor("pa_t2", [128, 512], F32))
            t_qst_free = [None, None]
            t_ev = {}
            n_q = 0
            for h in range(8):
                b = n_q % 2
                n_q += 1
                DVE.wait(t_qst_free[b])
                for qt in range(4):
                    bank = qt
                    PE.wait(t_ev.get(bank))
                    for c in range(8):
                        ins = nc.tensor.matmul(ps_all[0:64, bank, :], lhsT=wA[:, c, 64 * h:64 * h + 64],
                                               rhs=uo[:, c, qt * 512:(qt + 1) * 512], start=(c == 0), stop=(c == 7))
                    tk = PE.mark(ins)
                    DVE.wait(tk)
                    t_ev[bank] = DVE.mark(nc.vector.tensor_scalar(out=qst[b][0:64, qt * 512:(qt + 1) * 512], in0=ps_all[0:64, bank, :],
                                                                  scalar1=0.125, scalar2=None, op0=ALU.mult))
                POOL.wait(t_ev[3])
                t_qst_free[b] = st_q[b].add(nc.gpsimd.dma_start(out=Qf_scr[h, 0:64, :], in_=qst[b][0:64, :]))
            DVE.wait(t_cs)
            for h in range(4):
                b = n_q % 2
                n_q += 1
                DVE.wait(t_qst_free[b])
                for qt in range(4):
                    ba, bb = 4 + 2 * (qt % 2), 5 + 2 * (qt % 2)
                    PE.wait(t_ev.get(ba))
                    for c in range(8):
                        nc.tensor.matmul(ps_all[:, ba, :], lhsT=wA[:, c, 512 + 128 * h:512 + 128 * h + 128],
                                         rhs=uo[:, c, qt * 512:(qt + 1) * 512], start=(c == 0), stop=(c == 7))
                    for c in range(8):
                        ins = nc.tensor.matmul(ps_all[:, bb, :], lhsT=wA[:, c, 1024 + 128 * h:1024 + 128 * h + 128],
                                               rhs=uo[:, c, qt * 512:(qt + 1) * 512], start=(c == 0), stop=(c == 7))
                    tk = PE.mark(ins)
                    DVE.wait(tk)
                    DVE.wait(t_ev.get("tmp"))
                    nc.vector.tensor_tensor(out=tmp1[:], in0=ps_all[:, ba, :], in1=cq_sb[:, qt * 512:(qt + 1) * 512], op=ALU.mult)
                    tk = DVE.mark(nc.vector.tensor_tensor(out=tmp2[:], in0=ps_all[:, bb, :], in1=sq_sb[:, qt * 512:(qt + 1) * 512], op=ALU.mult))
                    t_ev[ba] = tk
                    DVE.wait(tk)
                    tk = DVE.mark(nc.vector.tensor_tensor(out=tmp1[:], in0=tmp1[:], in1=tmp2[:], op=ALU.add))
                    DVE.wait(tk)
                    tk = DVE.mark(nc.vector.tensor_scalar(out=qst[b][:, qt * 512:(qt + 1) * 512], in0=tmp1[:], scalar1=0.125,
                                                          scalar2=None, op0=ALU.mult))
                    t_ev["tmp"] = tk
                POOL.wait(tk)
                t_qst_free[b] = st_q[b].add(nc.gpsimd.dma_start(out=Qd_scr[h, :, :], in_=qst[b][:, :]))
            gst = [pa.enter_context(nc.sbuf_tensor(f"pa_gst{i}", [128, D], BF16)) for i in range(2)]
            t_gst_free = [None, None]
            t_gev = [None, None]
            PE.wait(t_ev[4])
            PE.wait(t_ev[6])
            PE.wait(t_ev["tmp"])
            for j in range(16):
                b = j % 2
                for half in range(2):
                    bank = half
                    PE.wait(t_gev[half])
                    for c in range(8):
                        ins = nc.tensor.matmul(ps_all[:, bank, :], lhsT=uo[:, c, j * 128:(j + 1) * 128],
                                               rhs=wA[:, c, 1536 + half * 512:1536 + (half + 1) * 512], start=(c == 0), stop=(c == 7))
                    tk = PE.mark(ins)
                    ACT.wait(tk)
                    ACT.wait(t_gst_free[b])
                    tk = ACT.mark(nc.scalar.activation(out=gst[b][:, half * 512:(half + 1) * 512], in_=ps_all[:, bank, :], func=AF.Silu))
                    t_gev[half] = tk
                POOL.wait(tk)
                t_gst_free[b] = st_gt[b].add(nc.gpsimd.dma_start(out=gate_scr[j * 128:(j + 1) * 128, :], in_=gst[b][:]))
            POOL.wait(t_gst_free[0])
            POOL.wait(t_gst_free[1])
            POOL.wait(t_qst_free[0])
            POOL.wait(t_qst_free[1])
            t_pa_end = POOL.mark(nc.gpsimd.memset(ones_bf[:, 0:1], 1.0))
        for e_ in (DVE, PE, ACT, SP):
            e_.wait(t_pa_end)
        full_barrier()
        for tk in t_scr_done:
            SP.wait(tk)

        ld_u = [DmaSem(newsem(f"ld_u{i}")) for i in range(3)]
        ld_cs = [DmaSem(newsem(f"ld_cs{i}")) for i in range(3)]
        ld_g = DmaSem(newsem("ld_g"))
        st_m = DmaSem(newsem("st_m"))
        gstate = {"step": 0, "block": 0, "ut_n": 0}
        t_exp = {}
        t_pv = {}
        t_epi = {}
        t_ut_free = [None, None, None]
        t_gsb_free = [None]

        def run_kind(kind):
            nu = 4 if kind == "fox" else 2
            NS = 3 if kind == "fox" else 2
            LA = NS - 1
            with ExitStack() as gs:
                def gsbuf(name, shape, dt):
                    return gs.enter_context(nc.sbuf_tensor(f"{kind}_{name}", list(shape), dt))
                if kind == "fox":
                    Kt = [[gsbuf(f"K{p}{i}", [65 if i % 2 == 0 else 128, 4096], BF16) for i in range(4)] for p in range(2)]
                    Vt = [gsbuf(f"V{p}", [128, 32, 4, 65], BF16) for p in range(2)]
                    Qt = [gsbuf(f"Q{i}", [65 if i % 2 == 0 else 128, NQ], BF16) for i in range(4)]
                    acc = gsbuf("acc", [128, 4, 4, 260], F32)
                    wk = [gsbuf(f"wk{g_}", [128, 8, 256], BF16) for g_ in range(2)]
                    wv = [gsbuf(f"wv{g_}", [128, 8, 256], BF16) for g_ in range(2)]
                    tot_sb = gsbuf("tot", [128, 4, 65], F32)
                    rden = gsbuf("rden", [128, 4], F32)
                else:
                    Kt = [[gsbuf(f"K{p}{i}", [128, 4096], BF16) for i in range(2)] for p in range(2)]
                    Vt = [gsbuf(f"V{p}", [128, 32, 2, 129], BF16) for p in range(2)]
                    Qt = [gsbuf(f"Q{i}", [128, NQ], BF16) for i in range(2)]
                    acc = gsbuf("acc", [128, 2, 4, 8 * 129], F32)
                    wk = [gsbuf(f"wk{g_}", [128, 8, 256], BF16) for g_ in range(2)]
                    wv = [gsbuf(f"wv{g_}", [128, 8, 256], BF16) for g_ in range(2)]
                    cst_ = [gsbuf(f"cos{i}", [128, 512], F32) for i in range(3)]
                    snt_ = [gsbuf(f"sin{i}", [128, 512], F32) for i in range(3)]
                    tmpa = gsbuf("tmpa", [128, 512], F32)
                    tmpb = gsbuf("tmpb", [128, 512], F32)
                    khi = gsbuf("khi", [128, 512], BF16)
                    klo = gsbuf("klo", [128, 512], BF16)
                    tot_sb = gsbuf("tot", [128, 8, 129], F32)
                    rden = gsbuf("rden", [128, 12], F32)
                    ya = gsbuf("ya", [128, 4, 128], F32)
                    ysq = gsbuf("ysq", [128, 128], F32)
                    ssq = gsbuf("ssq", [128, 8], F32)
                    rs = gsbuf("rs", [128, 8], F32)
                    yb = gsbuf("yb", [128, 128], F32)

                for G_ in range(2):
                    if kind == "fox":
                        load_w(wk[G_][:], w_in_v, C_FK + 256 * G_, 256)
                        t_w = load_w(wv[G_][:], w_in_v, C_FV + 256 * G_, 256)
                    else:
                        load_w(wk[G_][:], w_in_v, C_DK + 256 * G_, 256)
                        t_w = load_w(wv[G_][:], w_in_v, C_DV + 256 * G_, 256)
                if kind == "fox":
                    for i in (1, 3):
                        nc.vector.memset(Qt[i][0:64, :], 0.0)
                    for p in range(2):
                        for i in range(4):
                            if i % 2 == 0:
                                nc.vector.memset(Kt[p][i][64:65, :], 1.0)
                            else:
                                nc.vector.memset(Kt[p][i][0:64, :], 1.0)
                        tk = DVE.mark(nc.vector.memset(Vt[p][:, :, :, 64:65], 1.0))
                else:
                    for p in range(2):
                        tk = DVE.mark(nc.vector.memset(Vt[p][:, :, :, 128:129], 1.0))
                t_init = tk
                PE.wait(t_w)
                PE.wait(t_init)
                SP.wait(t_init)

                def load_group_inputs(G):
                    if kind == "fox":
                        for i in range(4):
                            if i % 2 == 0:
                                ld_g.add(nc.sync.dma_start(out=Qt[i][:, :], in_=Qf_scr[4 * G + i, :, :]))
                            else:
                                ld_g.add(nc.sync.dma_start(out=Qt[i][64:128, :], in_=Qf_scr[4 * G + i, 0:64, :]))
                                ld_g.add(nc.sync.dma_start(out=Qt[i][63:64, :], in_=Qf_scr[4 * G + i, 64:65, :]))
                        return ld_g.add(nc.sync.dma_start(out=gsb[:], in_=gate_scr[:, 256 * G:256 * G + 256].rearrange("(j q) c -> q j c", q=128)))
                    for i in range(2):
                        ld_g.add(nc.sync.dma_start(out=Qt[i][:, :], in_=Qd_scr[2 * G + i, :, :]))
                    return ld_g.add(nc.sync.dma_start(out=gsb[:], in_=gate_scr[:, 512 + 256 * G:512 + 256 * G + 256].rearrange("(j q) c -> q j c", q=128)))

                t_cs_free = [None, None, None]
                p1_done = {}
                att_last_pv = {}
                pst8 = {"kb": None, "vb": [None, None], "b7": None, "tmp": None}

                def p1_items(n):
                    G, cq = divmod(n, 4)
                    par = n % 2
                    loads = {}
                    if n >= 2:
                        DVE.wait(att_last_pv[n - 2])

                    def issue_ut(t):
                        n = gstate["ut_n"]
                        gstate["ut_n"] += 1
                        b = n % 3
                        SP.wait(t_ut_free[b])
                        T = 8 * cq + t
                        tk_ = ld_u[b].add(nc.sync.dma_start(out=utb[b][:], in_=uT_v[:, :, T * 512:(T + 1) * 512]))
                        tcs = None
                        if kind == "diff":
                            SP.wait(t_cs_free[b])
                            ld_cs[b].add(nc.sync.dma_start(out=cst_[b][:], in_=cosT[:, T * 512:(T + 1) * 512]))
                            tcs = ld_cs[b].add(nc.sync.dma_start(out=snt_[b][:], in_=sinT[:, T * 512:(T + 1) * 512]))
                        loads[t] = (b, tk_, tcs)

                    issue_ut(0)
                    issue_ut(1)
                    last_ev = None
                    for t in range(8):
                        if t + 2 < 8:
                            issue_ut(t + 2)
                        b, tk_, tcs = loads[t]
                        u = utb[b]
                        if kind == "fox":
                            for ip in range(2):
                                PE.wait(tk_)
                                PE.wait(pst8["kb"])
                                for c in range(8):
                                    ins = nc.tensor.matmul(ps_all[:, 3, :], lhsT=wk[G][:, c, 128 * ip:128 * ip + 128], rhs=u[:, c, :],
                                                           start=(c == 0), stop=(c == 7))
                                    if c == 3:
                                        yield
                                tk = PE.mark(ins)
                                DVE.wait(tk)
                                nc.vector.tensor_copy(out=Kt[par][2 * ip][0:64, t * 512:(t + 1) * 512], in_=ps_all[0:64, 3, :])
                                pst8["kb"] = DVE.mark(nc.vector.tensor_copy(out=Kt[par][2 * ip + 1][64:128, t * 512:(t + 1) * 512],
                                                                            in_=ps_all[64:128, 3, :]))
                                yield
                            for s_ in range(4):
                                PE.wait(pst8["vb"][s_ // 2])
                                ov = ps_all[:, 6 + s_ // 2, (s_ % 2) * 256:(s_ % 2) * 256 + 256]
                                for c in range(8):
                                    ins = nc.tensor.matmul(ov, lhsT=u[:, c, s_ * 128:(s_ + 1) * 128], rhs=wv[G][:, c, :],
                                                           start=(c == 0), stop=(c == 7))
                                    if c == 3:
                                        yield
                                tk = PE.mark(ins)
                                DVE.wait(tk)
                                ins = nc.vector.tensor_copy(out=Vt[par][:, 4 * t + s_, :, 0:64], in_=ov.rearrange("p (h d) -> p h d", d=64))
                                tkd = DVE.mark(ins)
                                pst8["vb"][s_ // 2] = tkd
                                last_ev = tkd
                                if s_ == 3:
                                    t_ut_free[b] = tk
                                yield
                        else:
                            for i in range(2):
                                PE.wait(tk_)
                                PE.wait(pst8["b7"])
                                for c in range(8):
                                    ins = nc.tensor.matmul(ps_all[:, 7, :], lhsT=wk[G][:, c, 128 * i:128 * i + 128], rhs=u[:, c, :],
                                                           start=(c == 0), stop=(c == 7))
                                    if c == 3:
                                        yield
                                tk = PE.mark(ins)
                                DVE.wait(tk)
                                DVE.wait(tcs)
                                DVE.wait(pst8["tmp"])
                                nc.vector.tensor_tensor(out=tmpa[:], in0=ps_all[:, 7, :], in1=cst_[b][:], op=ALU.mult)
                                tkh = DVE.mark(nc.vector.tensor_copy(out=khi[:], in_=ps_all[:, 7, :]))
                                DVE.wait(tkh)
                                tkl = DVE.mark(nc.vector.tensor_tensor(out=klo[:], in0=ps_all[:, 7, :], in1=khi[:], op=ALU.subtract))
                                yield
                                PE.wait(tkl)
                                nc.tensor.matmul(ps_all[:, 7, :], lhsT=permb[:, :], rhs=khi[:], start=True, stop=False)
                                tk = PE.mark(nc.tensor.matmul(ps_all[:, 7, :], lhsT=permb[:, :], rhs=klo[:], start=False, stop=True))
                                DVE.wait(tk)
                                tk2 = DVE.mark(nc.vector.tensor_tensor(out=tmpb[:], in0=ps_all[:, 7, :], in1=snt_[b][:], op=ALU.mult))
                                pst8["b7"] = tk2
                                DVE.wait(tk2)
                                pst8["tmp"] = DVE.mark(nc.vector.tensor_tensor(out=Kt[par][i][:, t * 512:(t + 1) * 512], in0=tmpa[:], in1=tmpb[:], op=ALU.add))
                                yield
                            t_cs_free[b] = pst8["tmp"]
                            for sp in range(2):
                                PE.wait(pst8["b7"])
                                for s_ in (2 * sp, 2 * sp + 1):
                                    ov = ps_all[:, 7, (s_ % 2) * 256:(s_ % 2) * 256 + 256]
                                    for c in range(8):
                                        ins = nc.tensor.matmul(ov, lhsT=u[:, c, s_ * 128:(s_ + 1) * 128], rhs=wv[G][:, c, :],
                                                               start=(c == 0), stop=(c == 7))
                                tk = PE.mark(ins)
                                DVE.wait(tk)
                                for s_ in (2 * sp, 2 * sp + 1):
                                    ov = ps_all[:, 7, (s_ % 2) * 256:(s_ % 2) * 256 + 256]
                                    ins = nc.vector.tensor_copy(out=Vt[par][:, 4 * t + s_, :, 0:128], in_=ov.rearrange("p (h d) -> p h d", d=128))
                                pst8["b7"] = DVE.mark(ins)
                                last_ev = pst8["b7"]
                                if sp == 1:
                                    t_ut_free[b] = tk
                                yield
                    toks = [last_ev, pst8["kb"], pst8["tmp"]]
                    p1_done[n] = [x for x in toks if x is not None]

                for _ in p1_items(0):
                    pass

                t_fin = [None]
                deferred = []
                t_store = [None]
                for n in range(8):
                    G, cq = divmod(n, 4)
                    par_kv = n % 2
                    if cq == 0:
                        if G == 1:
                            SP.wait(t_pv[gstate["step"] - 1])
                            SP.wait(t_store[0])
                            DVE.wait(t_fin[0])
                        tg = load_group_inputs(G)
                        PE.wait(tg)
                        DVE.wait(tg)
                    for v in p1_done[n]:
                        PE.wait(v)
                    Kc = Kt[par_kv]
                    Vc = Vt[par_kv]
                    units = [(g, hl) for g in range(cq, 4) for hl in range(nu)]
                    steps = []
                    for (g, hl) in units:
                        for r in range(32):
                            steps.append((g, hl, r))
                    nst = len(steps)
                    base_step = gstate["step"]
                    base_block = gstate["block"]

                    def geom(g, r):
                        diag = (g == cq)
                        m0 = (r // 8) if diag else 0
                        return diag, m0

                    def emit_qk(si):
                        g, hl, r = steps[si]
                        gi = base_step + si
                        diag, m0 = geom(g, r)
                        c0 = 128 * m0
                        slot = gi % NS
                        PE.wait(t_exp.get(gi - NS))
                        if kind == "fox":
                            bank = slot
                            p0_, p1_ = (0, 65) if hl % 2 == 0 else (0, 128)
                            ins = nc.tensor.matmul(ps_all[:, bank, c0:512], lhsT=Kc[hl][p0_:p1_, r * 128:(r + 1) * 128],
                                                   rhs=Qt[hl][p0_:p1_, g * 512 + c0:(g + 1) * 512], start=True, stop=not diag)
                            if diag:
                                ins = nc.tensor.matmul(ps_all[:, bank, c0:c0 + 128], lhsT=identb[:, :], rhs=masks[:, 0, r % 8, :],
                                                       start=False, stop=True)
                        else:
                            for mp in range(2):
                                bank = 2 * slot + mp
                                ins = nc.tensor.matmul(ps_all[:, bank, c0:512], lhsT=Kc[hl][64 * mp:64 * mp + 64, r * 128:(r + 1) * 128],
                                                       rhs=Qt[hl][64 * mp:64 * mp + 64, g * 512 + c0:(g + 1) * 512], start=True, stop=not diag)
                            if diag:
                                for mp in range(2):
                                    bank = 2 * slot + mp
                                    ins = nc.tensor.matmul(ps_all[:, bank, c0:c0 + 128], lhsT=identb[:, :], rhs=masks[:, 1, r % 8, :],
                                                           start=False, stop=True)
                        tqk = PE.mark(ins)
                        pb = gi % 3
                        ACT.wait(tqk)
                        ACT.wait(t_pv.get(gi - 3))
                        if kind == "fox":
                            kb = 32 * cq + r
                            ins = nc.scalar.activation(out=Pb[pb][:, 0, c0:512], in_=ps_all[:, slot, c0:512], func=AF.Exp,
                                                       bias=Gsb[:, 4 * G + hl, kb:kb + 1], scale=1.0)
                        else:
                            ins = nc.scalar.activation(out=Pb[pb][:, :, c0:512], in_=ps_all[:, 2 * slot:2 * slot + 2, c0:512], func=AF.Exp)
                        t_exp[gi] = ACT.mark(ins)

                    def acc_view(blk_par, mp, m):
                        if kind == "fox":
                            return ps_all[:, 4 + blk_par, m * 65:(m + 1) * 65]
                        a = mp * 4 + m
                        return ps_all[:, 4 + a // 3, (a % 3) * 129:(a % 3 + 1) * 129]

                    def emit_pv(si):
                        g, hl, r = steps[si]
                        gi = base_step + si
                        blk = base_block + si // 32
                        diag, m0 = geom(g, r)
                        pb = gi % 3
                        PE.wait(t_exp[gi])
                        if r == 0:
                            PE.wait(t_epi.get(blk - 2 if kind == "fox" else blk - 1))
                        par = blk % 2
                        nmap = 1 if kind == "fox" else 2
                        for mp in range(nmap):
                            for m in range(m0, 4):
                                last_r = (8 * m + 7) if diag else 31
                                if kind == "fox":
                                    rhs = Vc[:, r, hl, 0:65]
                                else:
                                    rhs = Vc[:, r, hl, 0:129]
                                first_in_bank = (m == 0) if kind == "fox" else ((mp * 4 + m) % 3 == 0)
                                ins = nc.tensor.matmul(acc_view(par, mp, m), lhsT=Pb[pb][:, mp, 128 * m:128 * m + 128], rhs=rhs,
                                                       start=(r == 0 and first_in_bank), stop=(r == last_r),
                                                       skip_group_check=True)
                        t_pv[gi] = PE.mark(ins)
                        if r == 31:
                            emit_epilogue(g, hl, blk, gi)

                    def emit_epilogue(g, hl, blk, gi_last):
                        DVE.wait(t_pv[gi_last])
                        par = blk % 2
                        final = (g == cq)
                        if kind == "fox":
                            O = ps_all[:, 4 + par, 0:260]
                            A = acc[:, hl, g, :]
                            if not final:
                                if cq == 0:
                                    ins = nc.vector.tensor_copy(out=A, in_=O)
                                else:
                                    ins = nc.vector.tensor_tensor(out=A, in0=A, in1=O, op=ALU.add)
                                t_epi[blk] = DVE.mark(ins)
                                return
                            T3 = tot_sb[:].rearrange("p m d -> p (m d)")
                            DVE.wait(t_fin[0])
                            if g > 0:
                                tk = DVE.mark(nc.vector.tensor_tensor(out=T3, in0=A, in1=O, op=ALU.add))
                            else:
                                tk = DVE.mark(nc.vector.tensor_copy(out=T3, in_=O))
                            t_epi[blk] = tk
                            DVE.wait(tk)
                            tk = DVE.mark(nc.vector.reciprocal(out=rden[:, 0:4], in_=tot_sb[:, :, 64]))
                            DVE.wait(tk)
                            for m in range(4):
                                j = 4 * g + m
                                gv = gsb[:, j, 64 * hl:64 * hl + 64]
                                ins = nc.vector.scalar_tensor_tensor(out=gv, in0=tot_sb[:, m, 0:64], scalar=rden[:, m:m + 1], in1=gv,
                                                                     op0=ALU.mult, op1=ALU.mult)
                            t_fin[0] = DVE.mark(ins)
                            return
                        A = acc[:, hl, g, :]
                        if not final:
                            for bnk in range(3):
                                w_ = 387 if bnk < 2 else 258
                                Ab = A[:, bnk * 387:bnk * 387 + w_]
                                Ob = ps_all[:, 4 + bnk, 0:w_]
                                if cq == 0:
                                    ins = nc.vector.tensor_copy(out=Ab, in_=Ob)
                                else:
                                    ins = nc.vector.tensor_tensor(out=Ab, in0=Ab, in1=Ob, op=ALU.add)
                            t_epi[blk] = DVE.mark(ins)
                            return
                        Tf = tot_sb[:].rearrange("p a d -> p (a d)")
                        DVE.wait(t_fin[0])
                        for bnk in range(3):
                            w_ = 387 if bnk < 2 else 258
                            Tb = Tf[:, bnk * 387:bnk * 387 + w_]
                            Ob = ps_all[:, 4 + bnk, 0:w_]
                            if g > 0:
                                ins = nc.vector.tensor_tensor(out=Tb, in0=A[:, bnk * 387:bnk * 387 + w_], in1=Ob, op=ALU.add)
                            else:
                                ins = nc.vector.tensor_copy(out=Tb, in_=Ob)
                        tk = DVE.mark(ins)
                        t_epi[blk] = tk
                        DVE.wait(tk)
                        tk = DVE.mark(nc.vector.reciprocal(out=rden[:, 0:8], in_=tot_sb[:, :, 128]))
                        DVE.wait(tk)
                        tk = DVE.mark(nc.vector.tensor_scalar(out=rden[:, 8:12], in0=rden[:, 4:8], scalar1=neglam[:, 0:1], scalar2=None, op0=ALU.mult))
                        DVE.wait(tk)
                        for m in range(4):
                            tk = DVE.mark(nc.vector.tensor_scalar(out=yb[:], in0=tot_sb[:, 4 + m, 0:128], scalar1=rden[:, 8 + m:9 + m],
                                                                  scalar2=None, op0=ALU.mult))
                            DVE.wait(tk)
                            tk = DVE.mark(nc.vector.scalar_tensor_tensor(out=ya[:, m, :], in0=tot_sb[:, m, 0:128], scalar=rden[:, m:m + 1],
                                                                         in1=yb[:], op0=ALU.mult, op1=ALU.add))
                            DVE.wait(tk)
                            tk = DVE.mark(nc.vector.tensor_tensor(out=ysq[:], in0=ya[:, m, :], in1=ya[:, m, :], op=ALU.mult))
                            DVE.wait(tk)
                            tk = DVE.mark(nc.vector.reduce_sum(out=ssq[:, m:m + 1], in_=ysq[:], axis=AX.X))
                        tk_ss = tk

                        def part2(g=g, hl=hl, tk_ss=tk_ss):
                            epi_part2(g, hl, tk_ss)
                        deferred.append([4, part2])

                    def epi_part2(g, hl, tk):
                        ACT.wait(tk)
                        tk = ACT.mark(nc.scalar.activation(out=ssq[:, 4:8], in_=ssq[:, 0:4], func=AF.Ln, bias=EPS, scale=1.0 / 128))
                        ACT.wait(tk)
                        tk = ACT.mark(nc.scalar.activation(out=rs[:, 0:4], in_=ssq[:, 4:8], func=AF.Exp, scale=-0.5))
                        DVE.wait(tk)
                        for m in range(4):
                            j = 4 * g + m
                            gv = gsb[:, j, 128 * hl:128 * hl + 128]
                            tk = DVE.mark(nc.vector.scalar_tensor_tensor(out=yb[:], in0=ya[:, m, :], scalar=rs[:, m:m + 1], in1=gsub[:, :],
                                                                         op0=ALU.mult, op1=ALU.mult))
                            DVE.wait(tk)
                            tk = DVE.mark(nc.vector.tensor_tensor(out=gv, in0=yb[:], in1=gv, op=ALU.mult))
                            DVE.wait(tk)
                        t_fin[0] = tk

                    gen = p1_items(n + 1) if n < 7 else None
                    n_items = 96 if kind == "fox" else 64
                    kint = max(1, nst // n_items)
                    for si in range(min(LA, nst)):
                        emit_qk(si)
                    for si in range(nst):
                        if si + LA < nst:
                            emit_qk(si + LA)
                        emit_pv(si)
                        for d_ in list(deferred):
                            d_[0] -= 1
                            if d_[0] <= 0:
                                deferred.remove(d_)
                                d_[1]()
                        if gen is not None and si % kint == kint - 1:
                            next(gen, None)
                    if gen is not None:
                        for _ in gen:
                            pass
                    gstate["step"] += nst
                    gstate["block"] += len(units)
                    att_last_pv[n] = t_pv[gstate["step"] - 1]
                    while deferred:
                        deferred.pop(0)[1]()
                    if cq == 3:
                        POOL.wait(t_fin[0])
                        POOL.wait(t_epi[gstate["block"] - 1])
                        col0 = 256 * G if kind == "fox" else 512 + 256 * G
                        t_store[0] = st_m.add(nc.gpsimd.dma_start(out=mixed_scr[:, col0:col0 + 256].rearrange("(j q) c -> q j c", q=128), in_=gsb[:]))


                POOL.wait(t_store[0])
                POOL.wait(t_pv[gstate["step"] - 1])
                POOL.wait(t_exp[gstate["step"] - 1])
                t_end = POOL.mark(nc.gpsimd.memset(ones_bf[:, 0:1], 1.0))
            for e_ in (DVE, PE, ACT, SP):
                e_.wait(t_end)
            full_barrier()

        with nc.allow_non_contiguous_dma(reason="512B row segments of gate/mixed scratch"):
            run_kind("fox")
            run_kind("diff")

        with ExitStack() as pc:
            wo = pc.enter_context(nc.sbuf_tensor("pc_wo", [128, 8, D], BF16))
            fgbc = pc.enter_context(nc.sbuf_tensor("pc_fgbc", [128, D], F32))
            ld_f = DmaSem(pc.enter_context(nc.semaphore("pc_ldf")))
            with nc.allow_non_contiguous_dma(reason="broadcast load"):
                t_fg = ld_f.add(nc.sync.dma_start(out=fgbc[:], in_=final_g.partition_broadcast(128)))
            DVE.wait(t_fg)
            load_w(wo[:, :, 0:512], w_out_v, 0, 512, scale=False)
            t_wo = load_w(wo[:, :, 512:1024], w_out_v, 512, 512, scale=False)
            PE.wait(t_wo)
            msb = [pc.enter_context(nc.sbuf_tensor(f"pc_m{i}", [128, D], BF16)) for i in range(2)]
            xsb = [pc.enter_context(nc.sbuf_tensor(f"pc_x{i}", [128, D], F32)) for i in range(2)]
            mT = [pc.enter_context(nc.sbuf_tensor(f"pc_mT{i}", [128, 8, 128], BF16)) for i in range(2)]
            hsb = [pc.enter_context(nc.sbuf_tensor(f"pc_h{i}", [128, D], F32)) for i in range(2)]
            hsq = pc.enter_context(nc.sbuf_tensor("pc_hsq", [128, D], F32))
            osb = [pc.enter_context(nc.sbuf_tensor(f"pc_o{i}", [128, D], F32)) for i in range(2)]
            stat = pc.enter_context(nc.sbuf_tensor("pc_stat", [128, 16, 4], F32))
            ld_m = [DmaSem(pc.enter_context(nc.semaphore(f"pc_ldm{i}"))) for i in range(2)]
            st_o = [DmaSem(pc.enter_context(nc.semaphore(f"pc_sto{i}"))) for i in range(2)]
            pst = ps_all[:, 7, :].bitcast(BF16)
            t_tr_ev = [None] * 16
            t_h = [None] * 16
            t_o = [None] * 16
            t_st = [None] * 16
            t_mm = [None] * 16
            t_ldm = [None] * 16

            def issue_c(j):
                b = j % 2
                if j >= 2:
                    SP.wait(t_h[j - 2])
                    SP.wait(t_tr_ev[j - 2])
                ld_m[b].add(nc.sync.dma_start(out=msb[b][:], in_=mixed_scr[j * 128:(j + 1) * 128, :]))
                t_ldm[j] = ld_m[b].add(nc.sync.dma_start(out=xsb[b][:], in_=xo[j * 128:(j + 1) * 128, :]))

            t_red = [None] * 16

            def stage_a(j):
                b = j % 2
                PE.wait(t_ldm[j])
                if j >= 1:
                    PE.wait(t_tr_ev[j - 1])
                for c in range(8):
                    ins = nc.tensor.transpose(pst[:, c * 128:(c + 1) * 128], msb[b][:, c * 128:(c + 1) * 128], identb[:, :])
                tk = PE.mark(ins)
                DVE.wait(tk)
                if j >= 2:
                    DVE.wait(t_mm[j - 2])
                t_tr_ev[j] = DVE.mark(nc.vector.tensor_copy(out=mT[b][:].rearrange("p c q -> p (c q)"), in_=pst))
                PE.wait(t_tr_ev[j])
                if j >= 2:
                    PE.wait(t_h[j - 2])
                for half in range(2):
                    for c in range(8):
                        ins = nc.tensor.matmul(ps_all[:, 2 * b + half, :], lhsT=mT[b][:, c, :], rhs=wo[:, c, half * 512:(half + 1) * 512],
                                               start=(c == 0), stop=(c == 7))
                t_mm[j] = PE.mark(ins)

            def stage_b(j):
                b = j % 2
                DVE.wait(t_mm[j])
                if j >= 2:
                    DVE.wait(t_o[j - 2])
                t_h[j] = DVE.mark(nc.vector.tensor_tensor(out=hsb[b][:], in0=ps_all[:, 2 * b:2 * b + 2, :].rearrange("p a f -> p (a f)"),
                                                          in1=xsb[b][:], op=ALU.add))
                POOL.wait(t_h[j])
                if j >= 1:
                    POOL.wait(t_red[j - 1])
                tk = POOL.mark(nc.gpsimd.tensor_tensor(out=hsq[:], in0=hsb[b][:], in1=hsb[b][:], op=ALU.mult))
                DVE.wait(tk)
                t_red[j] = DVE.mark(nc.vector.reduce_sum(out=stat[:, j, 0:1], in_=hsq[:], axis=AX.X))
                ACT.wait(t_red[j])
                tk = ACT.mark(nc.scalar.activation(out=stat[:, j, 1:2], in_=stat[:, j, 0:1], func=AF.Ln, bias=EPS, scale=1.0 / D))
                ACT.wait(tk)
                tk = ACT.mark(nc.scalar.activation(out=stat[:, j, 2:3], in_=stat[:, j, 1:2], func=AF.Exp, scale=-0.5))
                DVE.wait(tk)
                if j >= 2:
                    DVE.wait(t_st[j - 2])
                t_o[j] = DVE.mark(nc.vector.scalar_tensor_tensor(out=osb[b][:], in0=hsb[b][:], scalar=stat[:, j, 2:3], in1=fgbc[:, :],
                                                                 op0=ALU.mult, op1=ALU.mult))
                POOL.wait(t_o[j])
                t_st[j] = st_o[b].add(nc.gpsimd.dma_start(out=out[j * 128:(j + 1) * 128, :], in_=osb[b][:]))
                if j + 2 < 16:
                    issue_c(j + 2)

            issue_c(0)
            issue_c(1)
            stage_a(0)
            for j in range(16):
                if j + 1 < 16:
                    stage_a(j + 1)
                stage_b(j)
            POOL.wait(t_st[14])
            POOL.wait(t_st[15])
            t_fin_all = POOL.mark(nc.gpsimd.memset(ones_bf[:, 0:1], 1.0))
            for e_ in (DVE, PE, ACT, SP):
                e_.wait(t_fin_all)
    return nc


_CACHE = {}


def _host_consts():
    if "c" in _CACHE:
        return _CACHE["c"]
    inv_freq = (10000.0 ** (-np.arange(0, 64, 2, dtype=np.float32) / 64.0)).astype(np.float32)
    pos = np.arange(S, dtype=np.float32)
    ang = pos[None, :] * inv_freq[:, None]
    cos = np.cos(ang).astype(np.float32)
    sin = np.sin(ang).astype(np.float32)
    cos64 = np.concatenate([cos, cos], axis=0)
    sin64 = np.concatenate([-sin, sin], axis=0)
    cosT = np.ascontiguousarray(np.concatenate([cos64, cos64], axis=0))
    sinT = np.ascontiguousarray(np.concatenate([sin64, sin64], axis=0))
    consts = np.zeros((128, 4, 128), np.float32)
    consts[:, 0, :] = np.eye(128, dtype=np.float32)
    consts[:, 1, :] = 1.0
    consts[:, 2, :] = np.triu(np.ones((128, 128), np.float32))
    for k_ in range(128):
        sw = k_ + 32 if (k_ % 64) < 32 else k_ - 32
        consts[sw, 3, k_] = 1.0
    k = np.arange(128)[:, None]
    q = np.arange(128)[None, :]
    fox_tri = np.where(k <= q, 0.0, NEG).astype(np.float32)
    diff_tri = np.where((k // 64) <= (q // 64), 0.0, NEG).astype(np.float32)
    per_core = []
    for c in range(NCORES):
        sel = np.zeros((128, 16), np.float32)
        for j in range(16):
            sel[8 * j + c, j] = 1.0
        mf = np.zeros((128, 8, 128), np.float32)
        md = np.zeros((128, 8, 128), np.float32)
        for idx in range(8):
            if idx == c:
                mf[:, idx, :] = fox_tri
                md[:, idx, :] = diff_tri
            elif idx > c:
                mf[:, idx, :] = NEG
                md[:, idx, :] = NEG
        rows = np.concatenate([np.arange((8 * j + c) * 128, (8 * j + c + 1) * 128) for j in range(16)])
        per_core.append(dict(sel=sel, mfox=mf, mdiff=md, rows=rows,
                             cosq=np.ascontiguousarray(cosT[:, rows]), sinq=np.ascontiguousarray(sinT[:, rows])))
    _CACHE["c"] = (cosT, sinT, consts, per_core)
    return _CACHE["c"]


def _swap_halves(w):
    w4 = w.reshape(w.shape[0], 8, 2, 32)
    return np.ascontiguousarray(w4[:, :, ::-1, :].reshape(w.shape[0], 512))


def make_in_maps(x, norm_g, w_in, b_forget, lambda_q1, lambda_k1, lambda_q2, lambda_k2, subln_g, w_out, final_g):
    cosT, sinT, consts, per_core = _host_consts()
    x2 = np.asarray(x, np.float32).reshape(S, D)
    xT = np.ascontiguousarray(x2.T)
    w_in0 = np.ascontiguousarray(np.asarray(w_in, np.float32)[0])
    w_sw = np.ascontiguousarray(np.concatenate([_swap_halves(w_in0[:, C_DQ:C_DQ + 512]),
                                                _swap_halves(w_in0[:, C_DK:C_DK + 512])], axis=1))
    lam_in = np.ascontiguousarray(np.stack([np.asarray(lambda_q1, np.float32)[0], np.asarray(lambda_k1, np.float32)[0],
                                            np.asarray(lambda_q2, np.float32)[0], np.asarray(lambda_k2, np.float32)[0]]))
    common = dict(xT=xT, w_in=w_in0, w_sw=w_sw, w_out=np.ascontiguousarray(np.asarray(w_out, np.float32)[0]),
                  norm_g=np.ascontiguousarray(np.asarray(norm_g, np.float32)[0]),
                  final_g=np.ascontiguousarray(np.asarray(final_g, np.float32)),
                  b_forget=np.ascontiguousarray(np.asarray(b_forget, np.float32)[0]), lam_in=lam_in,
                  subln_g=np.ascontiguousarray(np.asarray(subln_g, np.float32)[0]),
                  cosT=cosT, sinT=sinT, consts=consts)
    in_maps = []
    for c in range(NCORES):
        pc = per_core[c]
        m = dict(common)
        m["xoT"] = np.ascontiguousarray(xT[:, pc["rows"]])
        m["xo"] = np.ascontiguousarray(x2[pc["rows"], :])
        m["cosq"] = pc["cosq"]
        m["sinq"] = pc["sinq"]
        m["sel"] = pc["sel"]
        m["mfox"] = pc["mfox"]
        m["mdiff"] = pc["mdiff"]
        in_maps.append(m)
    return in_maps


def kernel(x, norm_g, w_in, b_forget, lambda_q1, lambda_k1, lambda_q2, lambda_k2, subln_g, w_out, final_g):
    in_maps = make_in_maps(x, norm_g, w_in, b_forget, lambda_q1, lambda_k1, lambda_q2, lambda_k2, subln_g, w_out, final_g)
    nc = build()
    res = run_bass_kernel_spmd(nc, in_maps, core_ids=list(range(NCORES)))
    _, _, _, per_core = _host_consts()
    outp = np.empty((S, D), np.float32)
    for c in range(NCORES):
        outp[per_core[c]["rows"], :] = np.asarray(res.results[c]["out"], np.float32).reshape(NQ, D)
    return outp.reshape(1, S, D)
```
